# Optimizing a Trainium2 kernel written in Bass

```python
import math
import jax, jax.numpy as jnp
from jax import lax
import numpy as np

D_MODEL = 2048
BATCH = 1
SEQ = 8192
DEPTH = 2

N_META = 16
D_MIX = 2 * D_MODEL
EPS = 1e-6
POOL_WIDTH = D_MIX // 4
POOL_WINDOWS = (2, 4, 8, 16)
POOL_GROUP = POOL_WIDTH // 4
SSD_WIDTH = D_MIX // 2
SSD_HEADDIM = 64
SSD_HEADS = SSD_WIDTH // SSD_HEADDIM
SSD_GROUPS = 4
SSD_STATE = 128
SSD_CONV = 4
SSD_CHUNK = 128
SSD_CONV_DIM = SSD_WIDTH + 2 * SSD_GROUPS * SSD_STATE
ATTN_WIDTH = D_MIX // 4
ATTN_HEADDIM = 128
ATTN_HEADS = ATTN_WIDTH // ATTN_HEADDIM
ATTN_KV_HEADS = 2
IDX_HEADS = 16
IDX_HEADDIM = 64
INDEX_TOPK = 256
Q_BLOCK = 128
ROPE_THETA = 500000.0
ROPE_FRACTION = 4

SPLIT_SIZES = (POOL_WIDTH, POOL_WIDTH,
               SSD_WIDTH, SSD_CONV_DIM, SSD_HEADS,
               ATTN_HEADS * ATTN_HEADDIM,
               ATTN_KV_HEADS * ATTN_HEADDIM,
               ATTN_KV_HEADS * ATTN_HEADDIM,
               ATTN_WIDTH,
               IDX_HEADS * IDX_HEADDIM, IDX_HEADDIM, IDX_HEADS)
D_IN = sum(SPLIT_SIZES)

kernel_name = 'hybrid_pool_ssd_dsa_block'


def rmsnorm(x, w):
    xf = x.astype(jnp.float32)
    y = xf * lax.rsqrt(jnp.mean(xf * xf, axis=-1, keepdims=True) + EPS)
    return (y * w.astype(jnp.float32)).astype(x.dtype)


def rope_partial(x, pos):
    d = x.shape[-1]
    rot = d // ROPE_FRACTION
    half = rot // 2
    inv = jnp.power(jnp.float32(ROPE_THETA), -(jnp.arange(half, dtype=jnp.float32) * 2.0 / rot))
    ang = pos.astype(jnp.float32)[:, None] * inv[None, :]
    cos = jnp.cos(ang)[:, None, :]
    sin = jnp.sin(ang)[:, None, :]
    xf = x.astype(jnp.float32)
    x1 = xf[..., :half]
    x2 = xf[..., half:rot]
    out = jnp.concatenate([x1 * cos - x2 * sin, x2 * cos + x1 * sin, xf[..., rot:]], axis=-1)
    return out.astype(x.dtype)


def pool_mixer(v, pool_w, pool_scale):
    b, t, _ = v.shape
    vf = v.astype(jnp.float32)
    cs0 = jnp.pad(jnp.cumsum(vf, axis=1), ((0, 0), (1, 0), (0, 0)))
    hi = jnp.arange(1, t + 1)
    groups = []
    for g, w in enumerate(POOL_WINDOWS):
        lo = jnp.maximum(hi - w, 0)
        cnt = (hi - lo).astype(jnp.float32)[None, :, None]
        sl = slice(g * POOL_GROUP, (g + 1) * POOL_GROUP)
        groups.append((cs0[:, hi, sl] - cs0[:, lo, sl]) / cnt)
    pooled = jnp.stack(groups, axis=2)
    y = (pooled - vf.reshape(b, t, len(POOL_WINDOWS), POOL_GROUP)).astype(v.dtype)
    y = jnp.einsum('btgc,gcd->btgd', y, pool_w).reshape(b, t, POOL_WIDTH)
    return y * pool_scale


def causal_dwconv(x, w, bias):
    k = w.shape[0]
    t = x.shape[1]
    xp = jnp.pad(x, ((0, 0), (k - 1, 0), (0, 0)))
    out = bias
    for i in range(k):
        out = out + xp[:, i:i + t] * w[i]
    return out


def segsum_exp(a):
    l = a.shape[-1]
    cs = jnp.cumsum(a, axis=-1)
    diff = cs[..., :, None] - cs[..., None, :]
    mask = jnp.tril(jnp.ones((l, l), dtype=bool))
    return jnp.where(mask, jnp.exp(jnp.where(mask, diff, 0.0)), 0.0)


def ssd_chunked(x, dt, a, bm, cm):
    b, tp, h, p = x.shape
    g, n = bm.shape[2], bm.shape[3]
    r = h // g
    c = tp // SSD_CHUNK
    l = SSD_CHUNK
    x = x.reshape(b, c, l, g, r, p)
    dt = dt.reshape(b, c, l, g, r)
    bm = bm.reshape(b, c, l, g, n)
    cm = cm.reshape(b, c, l, g, n)
    xdt = x * dt[..., None]
    da = dt.transpose(0, 1, 3, 4, 2) * a.reshape(g, r)[None, None, :, :, None]
    cs = jnp.cumsum(da, axis=-1)
    lm = segsum_exp(da)
    cb = jnp.einsum('bclgn,bcsgn->bcgls', cm, bm)
    y_diag = jnp.einsum('bcgrls,bcsgrp->bclgrp', cb[:, :, :, None] * lm, xdt)
    decay_states = jnp.exp(cs[..., -1:] - cs)
    states = jnp.einsum('bcsgn,bcgrs,bcsgrp->bcgrpn', bm, decay_states, xdt)
    chunk_decay = jnp.exp(cs[..., -1])

    def step(hstate, inp):
        dec, st = inp
        return hstate * dec[..., None, None] + st, hstate

    h0 = jnp.zeros((b, g, r, p, n), jnp.float32)
    _, prev = lax.scan(step, h0, (jnp.swapaxes(chunk_decay, 0, 1), jnp.swapaxes(states, 0, 1)))
    prev = jnp.swapaxes(prev, 0, 1)
    y_off = jnp.einsum('bclgn,bcgrpn,bcgrl->bclgrp', cm, prev, jnp.exp(cs))
    return (y_diag + y_off).reshape(b, tp, h, p)


def pad_front(a, n):
    return jnp.pad(a, ((0, 0), (n, 0)) + ((0, 0),) * (a.ndim - 2))


def ssd_mixer(xbc_raw, z, dt_raw, conv_w, conv_b, dt_bias, a_log, d_skip, norm_w):
    b, t, _ = xbc_raw.shape
    xbc = jax.nn.silu(causal_dwconv(xbc_raw, conv_w, conv_b)).astype(jnp.float32)
    xs = xbc[..., :SSD_WIDTH].reshape(b, t, SSD_HEADS, SSD_HEADDIM)
    bm = xbc[..., SSD_WIDTH:SSD_WIDTH + SSD_GROUPS * SSD_STATE].reshape(b, t, SSD_GROUPS, SSD_STATE)
    cm = xbc[..., SSD_WIDTH + SSD_GROUPS * SSD_STATE:].reshape(b, t, SSD_GROUPS, SSD_STATE)
    dt = jax.nn.softplus(dt_raw.astype(jnp.float32) + dt_bias.astype(jnp.float32))
    a = -jnp.exp(a_log.astype(jnp.float32))
    pad = (SSD_CHUNK - N_META % SSD_CHUNK) % SSD_CHUNK
    y = ssd_chunked(pad_front(xs, pad), pad_front(dt, pad), a,
                    pad_front(bm, pad), pad_front(cm, pad))[:, pad:]
    y = y + d_skip.astype(jnp.float32)[:, None] * xs
    yg = (y.reshape(b, t, SSD_WIDTH) * jax.nn.silu(z.astype(jnp.float32))).reshape(b, t, SSD_GROUPS, -1)
    yg = yg * lax.rsqrt(jnp.mean(yg * yg, axis=-1, keepdims=True) + EPS)
    return (yg.reshape(b, t, SSD_WIDTH) * norm_w.astype(jnp.float32)).astype(xbc_raw.dtype)


def dsa_attention(q, k, v, q_idx, k_idx, w_idx, topk):
    b, t, h, d = q.shape
    kv = k.shape[2]
    rep = h // kv
    nb = -(-t // Q_BLOCK)
    tq = nb * Q_BLOCK
    scale = d ** -0.5
    idx_scale = (IDX_HEADDIM ** -0.5) * (IDX_HEADS ** -0.5)

    def blocks(a):
        a = jnp.pad(a, ((0, 0), (0, tq - t)) + ((0, 0),) * (a.ndim - 2))
        return jnp.swapaxes(a.reshape((b, nb, Q_BLOCK) + a.shape[2:]), 0, 1)

    key_pos = jnp.arange(t)
    k_idx_f = k_idx.astype(jnp.float32)
    gather = jax.vmap(lambda arr, ix: arr[ix])

    def one_block(args):
        i, qi, qii, wi = args
        q_pos = i * Q_BLOCK + jnp.arange(Q_BLOCK)
        s = jax.nn.relu(jnp.einsum('bqhd,bsd->bqhs', qii.astype(jnp.float32), k_idx_f))
        score = jnp.einsum('bqhs,bqh->bqs', s, wi.astype(jnp.float32)) * idx_scale
        causal = key_pos[None, :] <= q_pos[:, None]
        score = jnp.where(causal[None], score, -jnp.inf)
        _, sel = lax.top_k(score, topk)
        valid = sel <= q_pos[None, :, None]
        ks = gather(k, sel).astype(jnp.float32)
        vs = gather(v, sel).astype(jnp.float32)
        qg = qi.reshape(b, Q_BLOCK, kv, rep, d).astype(jnp.float32)
        logits = jnp.einsum('bqgrd,bqkgd->bqgrk', qg, ks) * scale
        logits = jnp.where(valid[:, :, None, None, :], logits, -jnp.inf)
        p = jax.nn.softmax(logits, axis=-1)
        o = jnp.einsum('bqgrk,bqkgd->bqgrd', p, vs)
        return o.reshape(b, Q_BLOCK, h, d).astype(q.dtype)

    out = lax.map(one_block, (jnp.arange(nb), blocks(q), blocks(q_idx), blocks(w_idx)))
    return jnp.swapaxes(out, 0, 1).reshape(b, tq, h, d)[:, :t]


def hybrid_layer(h, pos, topk, pre_w, post_w, w_in, pool_w, pool_scale, conv_w, conv_b,
                 dt_bias, a_log, d_skip, ssd_norm_w, w_out):
    b, t, _ = h.shape
    u = rmsnorm(h, pre_w)
    proj = jnp.einsum('btd,de->bte', u, w_in)
    offsets = np.cumsum(SPLIT_SIZES)[:-1].tolist()
    (pool_v, pool_g, ssd_z, ssd_xbc, ssd_dt, a_q, a_k, a_v, a_g,
     i_q, i_k, i_w) = jnp.split(proj, offsets, axis=-1)
    pool_out = jax.nn.silu(pool_g) * pool_mixer(pool_v, pool_w, pool_scale)
    ssd_out = ssd_mixer(ssd_xbc, ssd_z, ssd_dt, conv_w, conv_b, dt_bias, a_log, d_skip, ssd_norm_w)
    q = rope_partial(a_q.reshape(b, t, ATTN_HEADS, ATTN_HEADDIM), pos)
    k = rope_partial(a_k.reshape(b, t, ATTN_KV_HEADS, ATTN_HEADDIM), pos)
    v = a_v.reshape(b, t, ATTN_KV_HEADS, ATTN_HEADDIM)
    qi = rope_partial(i_q.reshape(b, t, IDX_HEADS, IDX_HEADDIM), pos)
    ki = rope_partial(i_k.reshape(b, t, 1, IDX_HEADDIM), pos)[:, :, 0]
    attn = dsa_attention(q, k, v, qi, ki, i_w, topk).reshape(b, t, ATTN_WIDTH)
    attn_out = jax.nn.silu(a_g) * attn
    mix = jnp.concatenate([pool_out, ssd_out, attn_out], axis=-1)
    out = jnp.einsum('bte,ed->btd', mix, w_out)
    return h + rmsnorm(out, post_w)


def setup_inputs(seed: int = 0) -> dict:
    key = jax.random.key(seed)
    ks = jax.random.split(key, 14)
    f = jnp.float32
    x = jax.random.normal(ks[0], (BATCH, SEQ, D_MODEL), f)
    meta_tokens = jax.random.normal(ks[1], (N_META, D_MODEL), f)
    pre_norm_w = 1.0 + 0.1 * jax.random.normal(ks[2], (DEPTH, D_MODEL), f)
    post_norm_w = 1.0 + 0.1 * jax.random.normal(ks[3], (DEPTH, D_MODEL), f)
    w_in = jax.random.normal(ks[4], (DEPTH, D_MODEL, D_IN), f) * D_MODEL ** -0.5
    pool_w = jax.random.normal(ks[5], (DEPTH, len(POOL_WINDOWS), POOL_GROUP, POOL_GROUP), f) * POOL_GROUP ** -0.5
    pool_scale = 1.0 + 0.1 * jax.random.normal(ks[6], (DEPTH, POOL_WIDTH), f)
    conv_w = jax.random.normal(ks[7], (DEPTH, SSD_CONV, SSD_CONV_DIM), f) * SSD_CONV ** -0.5
    conv_b = 0.01 * jax.random.normal(ks[8], (DEPTH, SSD_CONV_DIM), f)
    dt0 = jnp.exp(jax.random.uniform(ks[9], (DEPTH, SSD_HEADS), f)
                  * (math.log(0.1) - math.log(0.001)) + math.log(0.001))
    dt_bias = dt0 + jnp.log(-jnp.expm1(-dt0))
    a_log = jnp.log(jax.random.uniform(ks[10], (DEPTH, SSD_HEADS), f, minval=1.0, maxval=16.0))
    d_skip = 1.0 + 0.1 * jax.random.normal(ks[11], (DEPTH, SSD_HEADS), f)
    ssd_norm_w = 1.0 + 0.1 * jax.random.normal(ks[12], (DEPTH, SSD_WIDTH), f)
    w_out = jax.random.normal(ks[13], (DEPTH, D_MIX, D_MODEL), f) * D_MIX ** -0.5
    return {'x': x, 'meta_tokens': meta_tokens, 'pre_norm_w': pre_norm_w, 'post_norm_w': post_norm_w,
            'w_in': w_in, 'pool_w': pool_w, 'pool_scale': pool_scale, 'conv_w': conv_w, 'conv_b': conv_b,
            'dt_bias': dt_bias, 'a_log': a_log, 'd_skip': d_skip, 'ssd_norm_w': ssd_norm_w, 'w_out': w_out}


def reference(x, meta_tokens, pre_norm_w, post_norm_w, w_in, pool_w, pool_scale, conv_w, conv_b,
              dt_bias, a_log, d_skip, ssd_norm_w, w_out):
    b, s, d = x.shape
    meta = jnp.broadcast_to(meta_tokens.astype(x.dtype)[None], (b, N_META, d))
    h = jnp.concatenate([meta, x], axis=1)
    pos = jnp.arange(s + N_META)
    topk = min(INDEX_TOPK, s // 4)
    for l in range(DEPTH):
        h = hybrid_layer(h, pos, topk, pre_norm_w[l], post_norm_w[l], w_in[l], pool_w[l], pool_scale[l],
                         conv_w[l], conv_b[l], dt_bias[l], a_log[l], d_skip[l], ssd_norm_w[l], w_out[l])
    return h[:, N_META:]
```

```python
import contextlib
import numpy as np
import ml_dtypes
import concourse.bass as bass
import concourse.mybir as mybir
from concourse.bass_utils import run_bass_kernel_spmd

F32 = mybir.dt.float32
BF16 = mybir.dt.bfloat16
AF = mybir.ActivationFunctionType
ALU = mybir.AluOpType
AX = mybir.AxisListType

D = 2048
NK = 16
PADN = 112
NMETA = 16
EPS = 1e-6
NEG = -30000.0
NBIS = 18
TOPK = 256
NCORES = 8
SAME_ENGINE_SYNC = True

O_PV, O_PG, O_Z, O_X, O_B, O_C, O_DT = 0, 1024, 2048, 4096, 6144, 6656, 7168
O_Q, O_K, O_V, O_G, O_IQ, O_IK, O_IW = 7200, 8224, 8480, 8736, 9760, 10784, 10848


class View:
    def __init__(self, buf, ap):
        self.buf = buf
        self.ap = ap

    def map(self, f):
        return View(self.buf, f(self.ap))

    def __getitem__(self, idx):
        return View(self.buf, self.ap[idx])


class Buf:
    def __init__(self, t, name, dma_sem=None, ap0=None):
        self.t = t
        self.ap0 = ap0
        self.name = name
        self.w = []
        self.r = []
        self.dma_sem = dma_sem
        self.dma_cnt = 0
        self.psum = False

    def _base(self):
        return self.ap0 if self.ap0 is not None else self.t.ap()

    def __getitem__(self, idx):
        return View(self, self._base()[idx])

    def full(self):
        return View(self, self._base())

    def sub(self, name, idx):
        return Region(self, self._base()[idx])


class Region:
    def __init__(self, buf, ap0):
        self.buf = buf
        self.ap0 = ap0

    def __getitem__(self, idx):
        return View(self.buf, self.ap0[idx])

    def full(self):
        return View(self.buf, self.ap0)


class Prog:
    ENG = ('pe', 'dve', 'act', 'pool', 'sp')

    def __init__(self):
        self.nc = bass.Bass("TRN2", target_bir_lowering=False)
        nc = self.nc
        self.engs = {'pe': nc.tensor, 'dve': nc.vector, 'act': nc.scalar, 'pool': nc.gpsimd, 'sp': nc.sync}
        self.esem = {}
        self.ecnt = {}
        self.seen = {k: {} for k in self.engs}
        for k in ('pe', 'dve', 'act', 'pool'):
            self.esem[k] = nc.alloc_semaphore("e_" + k)
            self.ecnt[k] = 0
        self.st = contextlib.ExitStack()
        self.outs = []
        self.nwait = 0
        self.dmabufs = []
        self.sempool = []
        self.rec = None

    def _dsem(self, name, fresh=False):
        if self.sempool and not fresh:
            return self.sempool.pop()
        self.nsem = getattr(self, 'nsem', 0) + 1
        return (self.nc.alloc_semaphore("d_%d" % self.nsem), 0)

    def mark(self, label):
        if not hasattr(self, 'marks'):
            self.marks = []
        self.marks.append((label, dict(self.ecnt)))

    def barrier(self):
        deps = [(self.esem[k], self.ecnt[k]) for k in self.esem if self.ecnt[k] > 0]
        deps += [(b.dma_sem, b.dma_cnt) for b in self.dmabufs if b.dma_cnt > 0]
        for e in self.ENG:
            self._wait(e, deps)

    def sb(self, name, shape, dt, dma=False, stack=None):
        self.uid = getattr(self, 'uid', 0) + 1
        name = "%s_%d" % (name, self.uid)
        t = (stack or self.st).enter_context(self.nc.sbuf_tensor(name, list(shape), dt))
        b = Buf(t, name)
        if dma:
            b.dma_sem, b.dma_cnt = self._dsem(name)
            self.dmabufs.append(b)
            if stack is not None:
                def _rel(b=b):
                    self.sempool.append((b.dma_sem, b.dma_cnt))
                    self.dmabufs.remove(b)
                stack.callback(_rel)
        return b

    def ps(self, name, shape, dt, stack=None):
        self.uid = getattr(self, 'uid', 0) + 1
        name = "%s_%d" % (name, self.uid)
        t = (stack or self.st).enter_context(self.nc.psum_tensor(name, list(shape), dt))
        b = Buf(t, name)
        b.psum = True
        return b

    def din(self, name, shape, dt):
        return Buf(self.nc.dram_tensor(name, list(shape), dt, kind="ExternalInput"), name)

    def dout(self, name, shape, dt):
        b = Buf(self.nc.dram_tensor(name, list(shape), dt, kind="ExternalOutput"), name)
        b.dma_sem, b.dma_cnt = self._dsem(name, fresh=True)
        self.outs.append(b)
        self.dmabufs.append(b)
        return b

    def dint(self, name, shape, dt):
        b = Buf(self.nc.dram_tensor(name, list(shape), dt, kind="Internal"), name)
        b.dma_sem, b.dma_cnt = self._dsem(name, fresh=True)
        self.dmabufs.append(b)
        return b

    def _wait(self, ename, deps):
        eng = self.engs[ename]
        best = {}
        for (s, v) in deps:
            key = id(s)
            if key not in best or best[key][1] < v:
                best[key] = (s, v)
        for key, (s, v) in best.items():
            if ename == 'pe' and s is self.esem['pe']:
                continue
            if not SAME_ENGINE_SYNC and ename in self.esem and s is self.esem[ename]:
                continue
            if self.seen[ename].get(key, 0) >= v:
                continue
            eng.wait_ge(s, v)
            self.nwait += 1
            self.seen[ename][key] = v

    @staticmethod
    def _deps(reads, writes):
        deps = []
        for b in reads:
            deps += b.w
        for b in writes:
            deps += b.w + b.r
        return deps

    def I(self, ename, method, **kw):
        if self.rec is not None:
            self.rec.append(('I', ename, method, kw))
            return None
        reads, writes = [], []
        args = {}
        for k, v in kw.items():
            if isinstance(v, View):
                (writes if (k in ('out', 'accum_out', 'ap') or v.buf.psum) else reads).append(v.buf)
                args[k] = v.ap
            else:
                args[k] = v
        self._wait(ename, self._deps(reads, writes))
        inst = getattr(self.engs[ename], method)(**args)
        s = self.esem[ename]
        inst.then_inc(s, 1)
        self.ecnt[ename] += 1
        tok = (s, self.ecnt[ename])
        for b in reads:
            if b in writes:
                continue
            b.r = [t for t in b.r if t[0] is not s] + [tok]
        for b in writes:
            b.w = [tok]
            b.r = []
        return inst

    def replay(self, item):
        rec, self.rec = self.rec, None
        if item[0] == 'I':
            self.I(item[1], item[2], **item[3])
        else:
            self.dma(item[1], out=item[2], in_=item[3], **item[4])
        self.rec = rec

    def replay_merged(self, a, b):
        na, nb_ = len(a), len(b)
        j = 0
        for i_, it in enumerate(a):
            self.replay(it)
            tgt = ((i_ + 1) * nb_) // max(na, 1)
            while j < tgt:
                self.replay(b[j])
                j += 1
        while j < nb_:
            self.replay(b[j])
            j += 1

    def dma(self, q, out, in_, **kw):
        if self.rec is not None:
            self.rec.append(('D', q, out, in_, kw))
            return None
        dst, src = out.buf, in_.buf
        assert dst.dma_sem is not None, dst.name
        self._wait(q, self._deps([src], [dst]))
        inst = self.engs[q].dma_start(out=out.ap, in_=in_.ap, **kw)
        inst.then_inc(dst.dma_sem, 16)
        dst.dma_cnt += 16
        tok = (dst.dma_sem, dst.dma_cnt)
        src.r = [t for t in src.r if t[0] is not dst.dma_sem] + [tok]
        dst.w = [tok]
        dst.r = []
        return inst

    def finish(self):
        deps = []
        for b in self.outs:
            deps += b.w
        self._wait('sp', deps)
        self.st.close()
        return self.nc


def rsqrt_small(P, out, in_, scale, eps):
    P.I('dve', 'tensor_scalar', out=out, in0=in_, scalar1=scale, scalar2=eps, op0=ALU.mult, op1=ALU.add)
    P.I('act', 'activation', out=out, in_=out, func=AF.Sqrt)
    P.I('dve', 'reciprocal', out=out, in_=out)


def tiles_of(nb_total, step=4):
    out = []
    b = 0
    while b < nb_total:
        n = min(step, nb_total - b)
        out.append((b, n))
        b += n
    return out


def load_weights_bf16(P, wd, ncols, wbf, prew, stg, col0=0, colchunk=None):
    colchunk = colchunk or ncols
    i = 0
    for k in range(NK):
        for c0 in range(0, ncols, colchunk):
            cn = min(colchunk, ncols - c0)
            s = stg[i % len(stg)]
            P.dma('sp', out=s[:, 0:cn], in_=wd[k * 128:(k + 1) * 128, c0:c0 + cn])
            eng = 'dve' if i % 2 == 0 else 'pool'
            P.I(eng, 'tensor_scalar', out=wbf[:, k, col0 + c0:col0 + c0 + cn], in0=s[:, 0:cn],
                scalar1=prew[:, k:k + 1], scalar2=None, op0=ALU.mult)
            i += 1


def ranged_loader(wfull, ranges):
    def wld(P, wbf, prew, stg):
        i = 0
        width = stg[0].full().ap.shape[1]
        for k in range(NK):
            for (sc, n, dc) in ranges:
                for c0 in range(0, n, width):
                    cn = min(width, n - c0)
                    s_ = stg[i % len(stg)]
                    P.dma('sp', out=s_[:, 0:cn], in_=wfull[k * 128:(k + 1) * 128, sc + c0:sc + c0 + cn])
                    eng = 'dve' if i % 2 == 0 else 'pool'
                    P.I(eng, 'tensor_scalar', out=wbf[:, k, dc + c0:dc + c0 + cn], in0=s_[:, 0:cn],
                        scalar1=prew[:, k:k + 1], scalar2=None, op0=ALU.mult)
                    i += 1
    return wld


def load_ut_tile(P, ut, src, b0, nb, q='sp'):
    for k0 in range(0, NK, 4):
        for bb in range(nb):
            P.dma(q, out=ut[:, k0:k0 + 4, bb * 128:(bb + 1) * 128],
                  in_=src[b0 + bb, k0:k0 + 4].map(lambda a: a.rearrange("k d n -> d k n")))


def build_p1(NS):
    P = Prog()
    h = P.din("h", [NS, 128, D], F32)
    identd = P.din("ident", [128, 128], F32)
    uT = P.dout("uT", [NS, NK, 128, 128], BF16)
    emit_p1(P, NS, h, identd, uT)
    return P.finish()


def emit_p1(P, NS, h, identd, uT):
    with contextlib.ExitStack() as st:
        ident = P.sb("p1_ident", [128, 128], F32, dma=True, stack=st)
        identb = P.sb("p1_identb", [128, 128], BF16, stack=st)
        xt = [P.sb(f"p1_xt{i}", [128, D], F32, dma=True, stack=st) for i in range(2)]
        sq = P.sb("p1_sq", [128, D], BF16, stack=st)
        ss = P.sb("p1_ss", [128, 1], F32, stack=st)
        rs = P.sb("p1_rs", [128, 1], F32, stack=st)
        ub = [P.sb(f"p1_ub{i}", [128, D], BF16, stack=st) for i in range(2)]
        uTs = [P.sb(f"p1_uT{i}", [128, NK, 128], BF16, stack=st) for i in range(2)]
        pt = [P.ps(f"p1_pt{i}", [128, 4, 128], BF16, stack=st) for i in range(2)]
        P.dma('sp', out=ident[:], in_=identd.full())
        P.I('dve', 'tensor_copy', out=identb[:], in_=ident[:])
        for s in range(NS):
            xb = xt[s % 2]
            P.dma('sp', out=xb[:], in_=h[s])
            P.I('act', 'activation', out=sq[:], in_=xb[:], func=AF.Square, accum_out=ss[:])
            rsqrt_small(P, rs[:], ss[:], 1.0 / D, EPS)
            u = ub[s % 2]
            P.I('dve', 'tensor_scalar', out=u[:], in0=xb[:], scalar1=rs[:], scalar2=None, op0=ALU.mult)
            ut = uTs[s % 2]
            for k4 in range(NK // 4):
                p = pt[k4 % 2]
                for kk in range(4):
                    k = k4 * 4 + kk
                    P.I('pe', 'transpose', out=p[:, kk, :], in_=u[:, k * 128:(k + 1) * 128], identity=identb[:])
                if k4 % 2:
                    P.I('act', 'copy', out=ut[:, k4 * 4:(k4 + 1) * 4, :], in_=p[:])
                else:
                    P.I('dve', 'tensor_copy', out=ut[:, k4 * 4:(k4 + 1) * 4, :], in_=p[:])
            P.dma('pool', out=uT[s].map(lambda a: a.rearrange("k d n -> d k n")), in_=ut[:])
        P.barrier()


NCB = 1156


def build_p2(NB):
    P = Prog()
    PP = NB * 128
    io = dict(
        uT=P.din("uT", [NB, NK, 128, 128], BF16),
        wB=P.din("wB", [D, NCB], F32),
        prew=P.din("prew", [128, NK], F32),
        poolw=P.din("poolw", [256, 128], F32),
        pvec=P.din("pvec", [128, 32], F32),
        convw=P.din("convw", [128, 16], F32),
        rows=P.din("rows", [128, 520], F32),
        consts=P.din("consts", [128, 384], F32),
        mixB=P.dout("mixB", [3, 128, PP], BF16),
        ssq=P.dout("ssq", [NB, 128], F32),
    )
    emit_p2(P, NB, io)
    return P.finish()


def emit_p2(P, NB, io):
    P.mark("p2")
    PP = NB * 128
    uT, mixB, ssqd = io['uT'], io['mixB'], io['ssq']
    with contextlib.ExitStack() as st:
        sb = lambda n, s, d, dma=False: P.sb("p2_" + n, s, d, dma=dma, stack=st)
        ps = lambda n, s, d: P.ps("p2_" + n, s, d, stack=st)
        prew = sb("prew", [128, NK], F32, True)
        pvec = sb("pvec", [128, 32], F32, True)
        convw = sb("convw", [128, 16], F32, True)
        rows = sb("rows", [128, 520], F32, True)
        consts = sb("consts", [128, 384], F32, True)
        poolw_s = sb("poolw_s", [128, 2, 128], F32, True)
        poolw = sb("poolw", [128, 2, 128], BF16)
        identb = sb("identb", [128, 128], BF16)
        ones = sb("ones", [128, 128], F32)
        Aneg = sb("Aneg", [128, 4], F32)
        wbf = sb("wbf", [128, NK, NCB], BF16)
        stg = [sb(f"stg{i}", [128, NCB], F32, True) for i in range(2)]
        for t_, d_ in ((prew, io['prew']), (pvec, io['pvec']), (convw, io['convw']), (rows, io['rows']),
                       (consts, io['consts'])):
            P.dma('sp', out=t_[:], in_=d_.full())
        P.dma('sp', out=poolw_s[:], in_=io['poolw'].full().map(lambda a: a.rearrange("(c p) d -> p c d", p=128)))
        ident = consts[:, 0:128]
        U = consts[:, 128:256]
        MBT = consts[:, 256:384]
        P.I('dve', 'tensor_copy', out=poolw[:], in_=poolw_s[:])
        P.I('dve', 'tensor_copy', out=identb[:], in_=ident)
        P.I('dve', 'memset', ap=ones[:], constant=1.0)
        P.I('act', 'activation', out=Aneg[:], in_=rows[:, 4:8], func=AF.Exp)
        P.I('dve', 'tensor_scalar', out=Aneg[:], in0=Aneg[:], scalar1=-1.0, scalar2=None, op0=ALU.mult)
        io['wld'](P, wbf, prew, stg)
        dtb = rows[:, 0:4]
        dsk = rows[:, 8:264]
        nw = rows[:, 264:520]

        ut = [sb(f"ut{i}", [128, NK, 512], BF16, True) for i in range(3)]
        vraw = sb("vraw", [128, 2, 528], F32)
        xraw = sb("xraw", [128, 4, 515], F32)
        s2 = sb("s2", [128, 2, 528], F32)
        s4 = sb("s4", [128, 2, 528], F32)
        s8 = sb("s8", [128, 2, 528], F32)
        s16 = sb("s16", [128, 2, 528], F32)
        pacc = sb("pacc", [128, 2, 512], F32)
        ybf = sb("ybf", [128, 2, 512], BF16)
        sg = sb("sg", [128, 512], F32)
        cv = sb("cv", [128, 512], F32)
        xs_l = [sb(f"xs{i}", [128, 2, 512], F32) for i in range(2)]
        Bs_l = [sb(f"Bs{i}", [128, 512], BF16) for i in range(2)]
        Cs_l = [sb(f"Cs{i}", [128, 512], BF16) for i in range(2)]
        mixt = [sb(f"mixt{i}", [128, 3, 512], BF16) for i in range(2)]
        ssqc = sb("ssqc", [128, 128], F32)
        ssqT = sb("ssqT", [128, 128], F32)
        szs_l = [[sb(f"szs{i}{j}", [128, 256], F32) for j in range(4)] for i in range(2)]
        dt_l = [[sb(f"dt{i}{j}", [128, 4], F32) for j in range(4)] for i in range(2)]
        da_l = [[sb(f"da{i}{j}", [128, 4], F32) for j in range(4)] for i in range(2)]
        dab_l = [[sb(f"dab{i}{j}", [128, 4, 128], F32) for j in range(4)] for i in range(2)]
        negcs = sb("negcs", [128, 4], F32)
        ncsh = [sb(f"ncsh{i}", [128, 1], F32) for i in range(4)]
        ssq1 = sb("ssq1", [128, 1], F32)
        cl = sb("cl", [128, 4], F32)
        dte = sb("dte", [128, 4], F32)
        w2 = sb("w2", [128, 4], F32)
        ecs = sb("ecs", [128, 4], F32)
        cdec = sb("cdec", [128, 4], F32)
        CBTs = sb("CBTs", [128, 128], F32)
        LT = sb("LT", [128, 4, 128], F32)
        MT = sb("MT", [128, 4, 128], BF16)
        xdt = sb("xdt", [128, 256], BF16)
        xdtd = sb("xdtd", [128, 256], BF16)
        Bpm = sb("Bpm", [128, 128], BF16)
        ysb = sb("ysb", [128, 256], F32)
        ytmp = sb("ytmp", [128, 256], F32)
        ysq = sb("ysq", [128, 256], F32)
        ynb = sb("ynb", [128, 256], BF16)
        hT = sb("hT", [128, 256], F32)
        prevb = sb("prevb", [128, 256], BF16)

        pc = [ps(f"pc{i}", [128, 512], F32) for i in range(2)]
        pz = ps("pz", [128, 512], F32)
        csb = ps("csb", [128, 4, 128], F32)
        bkA = ps("bkA", [128, 512], F32)
        xpm = bkA.sub("xpm", (slice(None), slice(0, 256)))
        CBT = bkA.sub("CBT", (slice(None), slice(256, 384)))
        cspm = bkA.sub("cspm", (slice(None), slice(384, 388)))
        bkB = ps("bkB", [128, 512], F32)
        Yp = bkB.sub("Yp", (slice(None), slice(0, 256)))
        Yoff = bkB.sub("Yoff", (slice(None), slice(256, 512)))
        Stp = ps("Stp", [128, 256], F32)
        ptr = ps("ptr", [128, 3, 128], BF16)

        P.I('dve', 'memset', ap=vraw[:], constant=0.0)
        P.I('dve', 'memset', ap=xraw[:], constant=0.0)
        P.I('dve', 'memset', ap=hT[:], constant=0.0)
        P.I('dve', 'memset', ap=prevb[:], constant=0.0)
        P.I('dve', 'memset', ap=ssqc[:], constant=0.0)

        tl = tiles_of(NB)
        load_ut_tile(P, ut[0], uT, tl[0][0], tl[0][1])
        pci = 0
        recT, recC = [], []
        for ti, (b0, nb) in enumerate(tl):
            P.rec = []
            N = nb * 128
            u = ut[ti % 3]
            xs, Bs, Cs = xs_l[ti % 2], Bs_l[ti % 2], Cs_l[ti % 2]
            if ti + 1 < len(tl):
                load_ut_tile(P, ut[(ti + 1) % 3], uT, tl[ti + 1][0], tl[ti + 1][1])
            mx = mixt[ti % 2]

            def cproj(col0):
                nonlocal pci
                p = pc[pci % 2]
                pci += 1
                for k in range(NK):
                    P.I('pe', 'matmul', out=p[:, 0:N], lhsT=wbf[:, k, col0:col0 + 128], rhs=u[:, k, 0:N],
                        start=(k == 0), stop=(k == NK - 1))
                return p
            for cc in range(2):
                p = cproj(cc * 128)
                P.I('act', 'copy', out=vraw[:, cc, 16:16 + N], in_=p[:, 0:N])
            E = 16 + N
            P.I('pool', 'tensor_tensor', out=s2[:, :, 1:E], in0=vraw[:, :, 1:E], in1=vraw[:, :, 0:E - 1], op=ALU.add)
            P.I('pool', 'tensor_tensor', out=s4[:, :, 3:E], in0=s2[:, :, 3:E], in1=s2[:, :, 1:E - 2], op=ALU.add)
            P.I('pool', 'tensor_tensor', out=s8[:, :, 7:E], in0=s4[:, :, 7:E], in1=s4[:, :, 3:E - 4], op=ALU.add)
            P.I('pool', 'tensor_tensor', out=s16[:, :, 15:E], in0=s8[:, :, 15:E], in1=s8[:, :, 7:E - 8], op=ALU.add)
            P.I('dve', 'tensor_scalar', out=pacc[:, :, 0:N], in0=s2[:, :, 16:E], scalar1=pvec[:, 1:2], scalar2=None,
                op0=ALU.mult)
            for wi_, sw in ((1, s4), (2, s8), (3, s16)):
                P.I('dve', 'scalar_tensor_tensor', out=pacc[:, :, 0:N], in0=sw[:, :, 16:E],
                    scalar=pvec[:, 1 + wi_:2 + wi_], in1=pacc[:, :, 0:N], op0=ALU.mult, op1=ALU.add)
            if ti == 0:
                P.I('dve', 'tensor_tensor', out=pacc[:, :, PADN:128], in0=pacc[:, :, PADN:128],
                    in1=pvec[:, 5:21].map(lambda a: a.unsqueeze(1).broadcast_to([128, 2, 16])), op=ALU.mult)
            P.I('dve', 'tensor_tensor', out=ybf[:, :, 0:N], in0=pacc[:, :, 0:N], in1=vraw[:, :, 16:E], op=ALU.subtract)
            P.I('pool', 'tensor_copy', out=vraw[:, :, 0:16], in_=vraw[:, :, N:N + 16])
            p = cproj(256)
            P.I('act', 'activation', out=sg[:, 0:N], in_=p[:, 0:N], func=AF.Silu)
            p = pc[pci % 2]
            pci += 1
            for cc in range(2):
                P.I('pe', 'matmul', out=p[:, 0:N], lhsT=poolw[:, cc, :], rhs=ybf[:, cc, 0:N], start=(cc == 0),
                    stop=(cc == 1))
            P.I('dve', 'scalar_tensor_tensor', out=mx[:, 0, 0:N], in0=p[:, 0:N], scalar=pvec[:, 0:1], in1=sg[:, 0:N],
                op0=ALU.mult, op1=ALU.mult)
            for ci in range(4):
                p = cproj(384 + ci * 128)
                P.I('act', 'copy', out=xraw[:, ci, 3:3 + N], in_=p[:, 0:N])
                P.I('dve', 'tensor_scalar', out=cv[:, 0:N], in0=xraw[:, ci, 3:3 + N], scalar1=convw[:, ci * 4 + 3:ci * 4 + 4],
                    scalar2=pvec[:, 21 + ci:22 + ci], op0=ALU.mult, op1=ALU.add)
                for tp in range(3):
                    P.I('dve', 'scalar_tensor_tensor', out=cv[:, 0:N], in0=xraw[:, ci, tp:tp + N],
                        scalar=convw[:, ci * 4 + tp:ci * 4 + tp + 1], in1=cv[:, 0:N], op0=ALU.mult, op1=ALU.add)
                dst = xs[:, ci, 0:N] if ci < 2 else (Bs[:, 0:N] if ci == 2 else Cs[:, 0:N])
                P.I('act', 'activation', out=dst, in_=cv[:, 0:N], func=AF.Silu)
                P.I('pool', 'tensor_copy', out=xraw[:, ci, 0:3], in_=xraw[:, ci, N:N + 3])
            for bb in range(nb):
                blk = b0 + bb
                c0 = bb * 128
                cs_ = slice(c0, c0 + 128)
                szs, dt, da, dab = szs_l[ti % 2][bb], dt_l[ti % 2][bb], da_l[ti % 2][bb], dab_l[ti % 2][bb]
                for k in range(NK):
                    P.I('pe', 'matmul', out=pz[:, 0:260], lhsT=u[:, k, cs_], rhs=wbf[:, k, 896:1156],
                        start=(k == 0), stop=(k == NK - 1))
                P.I('act', 'activation', out=szs[:], in_=pz[:, 0:256], func=AF.Silu)
                P.I('dve', 'tensor_tensor', out=dt[:], in0=pz[:, 256:260], in1=dtb, op=ALU.add)
                P.I('act', 'activation', out=dt[:], in_=dt[:], func=AF.Exp)
                P.I('act', 'activation', out=dt[:], in_=dt[:], func=AF.Ln, bias=1.0)
                if blk == 0:
                    P.I('dve', 'memset', ap=dt[0:PADN, :], constant=0.0)
                P.I('dve', 'tensor_tensor', out=da[:], in0=dt[:], in1=Aneg[:], op=ALU.mult)
                P.I('pool', 'tensor_tensor', out=dab[:],
                    in0=ones[:].map(lambda a: a.unsqueeze(1).broadcast_to([128, 4, 128])),
                    in1=da[:].map(lambda a: a.unsqueeze(2).broadcast_to([128, 4, 128])), op=ALU.mult)
            recT.append(P.rec)
            P.rec = []
            for bb in range(nb):
                blk = b0 + bb
                c0 = bb * 128
                cs_ = slice(c0, c0 + 128)
                szs, dt, da, dab = szs_l[ti % 2][bb], dt_l[ti % 2][bb], da_l[ti % 2][bb], dab_l[ti % 2][bb]
                for cc in range(2):
                    P.I('pe', 'transpose', out=xpm[:, cc * 128:(cc + 1) * 128], in_=xs[:, cc, cs_], identity=ident)
                P.I('pe', 'transpose', out=ptr[:, 0, :], in_=Bs[:, cs_], identity=identb[:])
                P.I('act', 'copy', out=Bpm[:], in_=ptr[:, 0, :])
                P.I('pe', 'matmul', out=CBT[:], lhsT=Bs[:, cs_], rhs=Cs[:, cs_], start=True, stop=True)
                P.I('act', 'copy', out=CBTs[:], in_=CBT[:])
                P.I('pe', 'matmul', out=cspm[:], lhsT=U, rhs=da[:], start=True, stop=True)
                for hh in range(4):
                    P.I('pe', 'matmul', out=csb[:, hh, :], lhsT=dab[:, hh, :], rhs=U, start=True, stop=False)
                    P.I('pe', 'matmul', out=csb[:, hh, :], lhsT=ident, rhs=MBT, start=False, stop=True)
                P.I('dve', 'tensor_scalar', out=negcs[:], in0=cspm[:], scalar1=-1.0, scalar2=None, op0=ALU.mult)
                P.I('dve', 'tensor_copy', out=cl[:], in_=csb[:, :, 127])
                P.I('dve', 'tensor_tensor', out=dte[:], in0=negcs[:], in1=cl[:], op=ALU.add)
                P.I('act', 'activation', out=dte[:], in_=dte[:], func=AF.Exp)
                P.I('act', 'activation', out=ecs[:], in_=negcs[:], func=AF.Exp, scale=-1.0)
                P.I('act', 'activation', out=cdec[:], in_=cl[:], func=AF.Exp)
                for hh in range(4):
                    P.I('dve', 'tensor_scalar', out=ncsh[hh][:], in0=cspm[:, hh:hh + 1], scalar1=-1.0, scalar2=None, op0=ALU.mult)
                    P.I('act', 'activation', out=LT[:, hh, :], in_=csb[:, hh, :], func=AF.Exp, bias=ncsh[hh][:])
                P.I('pool', 'tensor_tensor', out=MT[:], in0=LT[:],
                    in1=CBTs[:].map(lambda a: a.unsqueeze(1).broadcast_to([128, 4, 128])), op=ALU.mult)
                P.I('dve', 'tensor_tensor', out=w2[:], in0=dt[:], in1=dte[:], op=ALU.mult)
                x3 = xpm[:].map(lambda a: a.rearrange("p (h d) -> p h d", h=4))
                P.I('dve', 'tensor_tensor', out=xdt[:].map(lambda a: a.rearrange("p (h d) -> p h d", h=4)), in0=x3,
                    in1=dt[:].map(lambda a: a.unsqueeze(2).broadcast_to([128, 4, 64])), op=ALU.mult)
                P.I('dve', 'tensor_tensor', out=xdtd[:].map(lambda a: a.rearrange("p (h d) -> p h d", h=4)), in0=x3,
                    in1=w2[:].map(lambda a: a.unsqueeze(2).broadcast_to([128, 4, 64])), op=ALU.mult)
                for hh in range(4):
                    P.I('pe', 'matmul', out=Yp[:, hh * 64:(hh + 1) * 64], lhsT=MT[:, hh, :], rhs=xdt[:, hh * 64:(hh + 1) * 64],
                        start=True, stop=True)
                P.I('pe', 'matmul', out=Yoff[:], lhsT=Cs[:, cs_], rhs=prevb[:], start=True, stop=True)
                P.I('pe', 'matmul', out=Stp[:], lhsT=Bpm[:], rhs=xdtd[:], start=True, stop=True)
                P.I('act', 'copy', out=ysb[:], in_=Yp[:])
                P.I('dve', 'tensor_tensor', out=ytmp[:].map(lambda a: a.rearrange("p (h d) -> p h d", h=4)),
                    in0=Yoff[:].map(lambda a: a.rearrange("p (h d) -> p h d", h=4)),
                    in1=ecs[:].map(lambda a: a.unsqueeze(2).broadcast_to([128, 4, 64])), op=ALU.mult)
                P.I('dve', 'tensor_tensor', out=ysb[:], in0=ysb[:], in1=ytmp[:], op=ALU.add)
                P.I('dve', 'tensor_tensor', out=ytmp[:], in0=xpm[:], in1=dsk, op=ALU.mult)
                P.I('dve', 'tensor_tensor', out=ysb[:], in0=ysb[:], in1=ytmp[:], op=ALU.add)
                P.I('dve', 'tensor_tensor', out=ysb[:], in0=ysb[:], in1=szs[:], op=ALU.mult)
                P.I('act', 'activation', out=ysq[:], in_=ysb[:], func=AF.Square, accum_out=ssq1[:])
                P.I('dve', 'tensor_copy', out=ssqc[:, blk:blk + 1], in_=ssq1[:])
                P.I('pool', 'tensor_tensor', out=ynb[:], in0=ysb[:], in1=nw, op=ALU.mult)
                for cc in range(2):
                    P.I('pe', 'transpose', out=ptr[:, 1 + cc, :], in_=ynb[:, cc * 128:(cc + 1) * 128], identity=identb[:])
                P.I('act', 'copy', out=mx[:, 1:3, cs_], in_=ptr[:, 1:3, :])
                P.I('dve', 'tensor_tensor', out=hT[:].map(lambda a: a.rearrange("p (h d) -> p h d", h=4)),
                    in0=hT[:].map(lambda a: a.rearrange("p (h d) -> p h d", h=4)),
                    in1=cdec[:].map(lambda a: a.unsqueeze(2).broadcast_to([128, 4, 64])), op=ALU.mult)
                P.I('dve', 'tensor_tensor', out=hT[:], in0=hT[:], in1=Stp[:], op=ALU.add)
                P.I('pool', 'tensor_copy', out=prevb[:], in_=hT[:])
            P.dma('pool', out=mixB[:, :, b0 * 128:b0 * 128 + N].map(lambda a: a.rearrange("c p n -> p c n")),
                  in_=mx[:, :, 0:N])
            recC.append(P.rec)
            P.rec = None
        P.replay_merged(recT[0], [])
        for ti in range(len(tl)):
            P.replay_merged(recC[ti], recT[ti + 1] if ti + 1 < len(tl) else [])
        P.I('pe', 'transpose', out=pz[:, 0:128], in_=ssqc[:], identity=ident)
        P.I('dve', 'tensor_copy', out=ssqT[:], in_=pz[:, 0:128])
        P.dma('pool', out=ssqd.full(), in_=ssqT[0:NB, :])
        P.barrier()


NCA = 3088
NCKV = 640


def slot_blocks_max(NB):
    NJ = (NB - 1) // 8
    return [0] + [8 * j + 8 for j in range(NJ)]


def rope(P, X, H, Dh, hf, cos, sin, tmp):
    x3 = X.map(lambda a: a.rearrange("p (h d) -> p h d", h=H))
    a_ = x3[:, :, 0:hf]
    b_ = x3[:, :, hf:2 * hf]
    r3 = lambda v: v.map(lambda a: a.rearrange("p (h d) -> p h d", h=H))
    c3, s3 = r3(cos), r3(sin)
    t1, t2, t3, t4 = [r3(t) for t in tmp]
    P.I('dve', 'tensor_tensor', out=t1, in0=a_, in1=c3, op=ALU.mult)
    P.I('dve', 'tensor_tensor', out=t2, in0=b_, in1=s3, op=ALU.mult)
    P.I('pool', 'tensor_tensor', out=t3, in0=a_, in1=s3, op=ALU.mult)
    P.I('pool', 'tensor_tensor', out=t4, in0=b_, in1=c3, op=ALU.mult)
    P.I('dve', 'tensor_tensor', out=a_, in0=t1, in1=t2, op=ALU.subtract)
    P.I('dve', 'tensor_tensor', out=b_, in0=t3, in1=t4, op=ALU.add)


def build_p3(NB):
    P = Prog()
    NS = 1 + (NB - 1) // 8
    PP = NB * 128
    io = dict(
        uT=P.din("uT", [NB, NK, 128, 128], BF16),
        uTo=P.din("uTo", [NS, NK, 128, 128], BF16),
        wA=P.din("wA", [D, NCA], F32),
        wKV=P.din("wKV", [D, NCKV], F32),
        prew=P.din("prew", [128, NK], F32),
        ropek=P.din("ropek", [NB, 128, 96], F32),
        ropeq=P.din("ropeq", [NS, 128, 512], F32),
        qpos=P.din("qpos", [128, NS], F32),
        kpos=P.din("kpos", [128, PP], F32),
        consts=P.din("consts", [128, 128], F32),
        attnT=P.dout("attnT", [NS, 8, 128, 128], BF16),
    )
    emit_p3(P, NB, io)
    return P.finish()


def emit_p3(P, NB, io, NS, sbm, tag=""):
    PP = NB * 128
    qTd = P.dint("p3_qTd" + tag, [NS, 128, 8, 128], BF16)
    qiTd = P.dint("p3_qiTd" + tag, [NS, 128, 8, 128], BF16)
    sgd = P.dint("p3_sgd" + tag, [NS, 128, 1024], BF16)
    wid = P.dint("p3_wid" + tag, [NS, 128, 16], F32)
    with contextlib.ExitStack() as st:
        sb = lambda n, s, d, dma=False, stack=st: P.sb("p3_" + n, s, d, dma=dma, stack=stack)
        prew = sb("prew", [128, NK], F32, True)
        consts = sb("consts", [128, 128], F32, True)
        identb = sb("identb", [128, 128], BF16)
        I4 = sb("I4", [128, 4, 128], BF16)
        P.dma('sp', out=prew[:], in_=io['prew'].full())
        P.dma('sp', out=consts[:], in_=io['consts'].full())
        P.I('dve', 'tensor_copy', out=identb[:], in_=consts[:])
        for i in range(4):
            P.I('dve', 'tensor_copy', out=I4[:, i, :], in_=consts[:])
        P.mark("p3a_A")
        with contextlib.ExitStack() as sa:
            sba = lambda n, s, d, dma=False: sb(n, s, d, dma, sa)
            wabf = sba("wabf", [128, NK, NCA], BF16)
            stg = [sba(f"stga{i}", [128, 1024], F32, True) for i in range(2)]
            io['wldA'](P, wabf, prew, stg)
            uto = [sba(f"uto{i}", [128, NK, 128], BF16, True) for i in range(2)]
            rq = [sba(f"rq{i}", [128, 512], F32, True) for i in range(2)]
            qpm = sba("qpm", [128, 1024], F32)
            ipm = sba("ipm", [128, 1024], F32)
            qb = sba("qb", [128, 1024], BF16)
            ib = sba("ib", [128, 1024], BF16)
            sgt = sba("sgt", [128, 1024], BF16)
            wit = sba("wit", [128, 16], F32)
            tmps = [sba(f"rt{i}", [128, 128], F32) for i in range(4)]
            qTs = sba("qTs", [128, 8, 128], BF16)
            qiTs = sba("qiTs", [128, 8, 128], BF16)
            pa = [P.ps(f"p3_pa{i}", [128, 512], F32, stack=sa) for i in range(4)]
            ptq = [P.ps(f"p3_ptq{i}", [128, 8, 128], BF16, stack=sa) for i in range(2)]
            pai = 0
            for s in range(NS):
                u = uto[s % 2]
                r = rq[s % 2]
                P.dma('sp', out=u[:], in_=io['uTo'][s].map(lambda a: a.rearrange("k d n -> d k n")))
                P.dma('sp', out=r[:], in_=io['ropeq'][s])

                def aproj(col0, n):
                    nonlocal pai
                    p = pa[pai % 4]
                    pai += 1
                    for k in range(NK):
                        P.I('pe', 'matmul', out=p[:, 0:n], lhsT=u[:, k, :], rhs=wabf[:, k, col0:col0 + n],
                            start=(k == 0), stop=(k == NK - 1))
                    return p
                for t in range(2):
                    p = aproj(t * 512, 512)
                    P.I('act', 'copy', out=qpm[:, t * 512:(t + 1) * 512], in_=p[:])
                for t in range(2):
                    p = aproj(1024 + t * 512, 512)
                    P.I('act', 'activation', out=sgt[:, t * 512:(t + 1) * 512], in_=p[:], func=AF.Silu)
                for t in range(2):
                    p = aproj(2048 + t * 512, 512)
                    P.I('act', 'copy', out=ipm[:, t * 512:(t + 1) * 512], in_=p[:])
                p = aproj(3072, 16)
                P.I('dve', 'tensor_copy', out=wit[:], in_=p[:, 0:16])
                rope(P, qpm[:], 8, 128, 16, r[:, 0:128], r[:, 128:256], [t[:] for t in tmps])
                rope(P, ipm[:], 16, 64, 8, r[:, 256:384], r[:, 384:512], [t[:] for t in tmps])
                P.I('dve', 'tensor_copy', out=qb[:], in_=qpm[:])
                P.I('pool', 'tensor_copy', out=ib[:], in_=ipm[:])
                for hh in range(8):
                    P.I('pe', 'transpose', out=ptq[0][:, hh, :], in_=qb[:, hh * 128:(hh + 1) * 128], identity=identb[:])
                P.I('dve', 'tensor_copy', out=qTs[:], in_=ptq[0][:])
                for hh in range(8):
                    P.I('pe', 'transpose', out=ptq[1][:, hh, :], in_=ib[:, hh * 128:(hh + 1) * 128], identity=identb[:])
                P.I('act', 'copy', out=qiTs[:], in_=ptq[1][:])
                P.dma('pool', out=qTd[s], in_=qTs[:])
                P.dma('pool', out=qiTd[s], in_=qiTs[:])
                P.dma('pool', out=sgd[s], in_=sgt[:])
                P.dma('pool', out=wid[s], in_=wit[:])
            P.barrier()
        P.mark("p3a_KV")
        kT = sb("kT", [128, 2, PP], BF16)
        Vx = sb("Vx", [128, NB, 2, 129], BF16)
        kiT = sb("kiT", [128, PP], BF16)
        P.I('pool', 'memset', ap=Vx[:], constant=1.0)
        with contextlib.ExitStack() as sk:
            sbk = lambda n, s, d, dma=False: sb(n, s, d, dma, sk)
            wkbf = sbk("wkbf", [128, NK, NCKV], BF16)
            stg = [sbk(f"stgk{i}", [128, NCKV], F32, True) for i in range(2)]
            io['wldKV'](P, wkbf, prew, stg)
            ut = [sbk(f"ut{i}", [128, NK, 512], BF16, True) for i in range(2)]
            rk = [sbk(f"rk{i}", [128, 4, 96], F32, True) for i in range(2)]
            kpm = sbk("kpm", [128, 256], F32)
            kipm = sbk("kipm", [128, 128], F32)
            kb = sbk("kb", [128, 384], BF16)
            tmps = [sbk(f"rtk{i}", [128, 32], F32) for i in range(4)]
            pk = [P.ps(f"p3_pk{i}", [128, 512], F32, stack=sk) for i in range(2)]
            pk2 = [P.ps(f"p3_pk2{i}", [128, 128], F32, stack=sk) for i in range(2)]
            ptk = P.ps("p3_ptk", [128, 3, 128], BF16, stack=sk)
            tl = tiles_of(NB)

            def ldk(ti):
                b0, nb = tl[ti]
                load_ut_tile(P, ut[ti % 2], io['uT'], b0, nb)
                P.dma('sp', out=rk[ti % 2][:, 0:nb, :], in_=io['ropek'][b0:b0 + nb].map(lambda a: a.rearrange("b p c -> p b c")))
            ldk(0)
            it = 0
            for ti, (b0, nb) in enumerate(tl):
                if ti + 1 < len(tl):
                    ldk(ti + 1)
                u = ut[ti % 2]
                r = rk[ti % 2]
                for bb in range(nb):
                    blk = b0 + bb
                    cs_ = slice(bb * 128, bb * 128 + 128)
                    p1, p2 = pk[it % 2], pk2[it % 2]
                    it += 1
                    for k in range(NK):
                        P.I('pe', 'matmul', out=p1[:], lhsT=u[:, k, cs_], rhs=wkbf[:, k, 0:512], start=(k == 0), stop=(k == NK - 1))
                    for k in range(NK):
                        P.I('pe', 'matmul', out=p2[:], lhsT=u[:, k, cs_], rhs=wkbf[:, k, 512:640], start=(k == 0), stop=(k == NK - 1))
                    P.I('act', 'copy', out=kpm[:], in_=p1[:, 0:256])
                    P.I('act', 'copy', out=Vx[:, blk, :, 0:128], in_=p1[:, 256:512].map(lambda a: a.rearrange("p (g d) -> p g d", g=2)))
                    P.I('act', 'copy', out=kipm[:], in_=p2[:])
                    rope(P, kpm[:], 2, 128, 16, r[:, bb, 0:32], r[:, bb, 32:64], [t[:] for t in tmps])
                    rope(P, kipm[:], 2, 64, 8, r[:, bb, 64:80], r[:, bb, 80:96], [t[:, 0:16] for t in tmps])
                    P.I('dve', 'tensor_copy', out=kb[:, 0:256], in_=kpm[:])
                    P.I('pool', 'tensor_copy', out=kb[:, 256:384], in_=kipm[:])
                    for i in range(3):
                        P.I('pe', 'transpose', out=ptk[:, i, :], in_=kb[:, i * 128:(i + 1) * 128], identity=identb[:])
                    P.I('dve', 'tensor_copy', out=kT[:, :, blk * 128:(blk + 1) * 128], in_=ptk[:, 0:2, :])
                    P.I('act', 'copy', out=kiT[:, blk * 128:(blk + 1) * 128], in_=ptk[:, 2, :])
            P.barrier()
        P.mark("p3b")
        with contextlib.ExitStack() as sq_:
            sbq = lambda n, s, d, dma=False: sb(n, s, d, dma, sq_)
            qpos = sbq("qpos", [128, NS], F32, True)
            kcol = sbq("kcol", [128, 1024], F32, True)
            P.dma('sp', out=qpos[:], in_=io['qpos'].full())
            P.dma('sp', out=kcol[:], in_=io['kpos'][:, 0:1024])
            qT = [sbq(f"qT{i}", [128, 8, 128], BF16, True) for i in range(2)]
            qiT = [sbq(f"qiT{i}", [128, 8, 128], BF16, True) for i in range(2)]
            sgq = [sbq(f"sgq{i}", [128, 1024], BF16, True) for i in range(2)]
            wiq = [sbq(f"wiq{i}", [128, 16], F32, True) for i in range(2)]
            Wd = sbq("Wd", [128, 16, 128], BF16)
            pw2 = sbq("pw2", [128, NBIS + 1], F32)
            tabA = sbq("tabA", [128, NBIS + 1], F32)
            tabB = sbq("tabB", [128, NBIS + 1], F32)
            for k_ in range(NBIS + 1):
                P.I('dve', 'memset', ap=pw2[:, k_:k_ + 1], constant=float(2.0 ** -k_))
            score = sbq("score", [128, PP], F32)
            MBs = [sbq(f"MB{i}", [128, PP], BF16) for i in range(2)]
            pen = sbq("pen", [128, 1024], F32)
            R = [sbq(f"R{i}", [128, 512], BF16) for i in range(8)]
            PT = [sbq(f"PT{i}", [128, 512], BF16) for i in range(4)]
            att = sbq("att", [128, 1024], BF16)
            attT = [sbq(f"attT{i}", [128, 8, 128], BF16) for i in range(2)]
            col = {n: sbq("c_" + n, [128, 1], F32) for n in ("m", "lo", "wh", "mid", "cnt", "g", "rinv", "qrel")}
            OA = P.ps("p3_OA", [128, 512], F32, stack=sq_)
            OB = P.ps("p3_OB", [128, 512], F32, stack=sq_)
            Osl = [OA[:, 0:129], OA[:, 129:258], OA[:, 258:387], OB[:, 0:129]]
            Lp = P.ps("p3_L", [128, 512], F32, stack=sq_)
            Sb = [P.ps(f"p3_S{i}", [128, 512], F32, stack=sq_) for i in range(4)]
            Sall = Sb + [OA, OB, Lp]
            Lall = [Lp] + Sb
            SC = P.ps("p3_SC", [128, 512], F32, stack=sq_)
            scale = 128 ** -0.5
            cnt_ = dict(si=0, ri=0, li=0)

            def ldq(s):
                i = s % 2
                P.dma('sp', out=qT[i][:], in_=qTd[s])
                P.dma('sp', out=qiT[i][:], in_=qiTd[s])
                P.dma('sp', out=sgq[i][:], in_=sgd[s])
                P.dma('sp', out=wiq[i][:], in_=wid[s])

            def nkb_of(s):
                return min(sbm[s] + 1, NB)

            def indexer(s):
                i = s % 2
                nkb = nkb_of(s)
                P.I('dve', 'tensor_tensor', out=Wd[:],
                    in0=identb[:].map(lambda a: a.unsqueeze(1).broadcast_to([128, 16, 128])),
                    in1=wiq[i][:].map(lambda a: a.unsqueeze(2).broadcast_to([128, 16, 128])), op=ALU.mult)
                for (t0, tn) in tiles_of(nkb):
                    c0, n = t0 * 128, tn * 128
                    pend = None
                    for pr_ in range(9):
                        cur = None
                        if pr_ < 8:
                            cur = []
                            for j in range(2):
                                hd = 2 * pr_ + j
                                p = Sall[cnt_['si'] % 7]
                                cnt_['si'] += 1
                                lo_ = j * 64
                                P.I('pe', 'matmul', out=p[:, 0:n], lhsT=qiT[i][lo_:lo_ + 64, pr_, :], rhs=kiT[lo_:lo_ + 64, c0:c0 + n],
                                    start=True, stop=True)
                                cur.append((hd, p))
                        if pend is not None:
                            for (ph, r) in pend:
                                P.I('pe', 'matmul', out=SC[:, 0:n], lhsT=Wd[:, ph, :], rhs=r[:, 0:n], start=(ph == 0), stop=(ph == 15))
                        pend = None
                        if cur is not None:
                            pend = []
                            for j, (hd, p) in enumerate(cur):
                                r = R[cnt_['ri'] % 8]
                                cnt_['ri'] += 1
                                if j == 0:
                                    P.I('act', 'activation', out=r[:, 0:n], in_=p[:, 0:n], func=AF.Relu)
                                else:
                                    P.I('dve', 'tensor_scalar', out=r[:, 0:n], in0=p[:, 0:n], scalar1=0.0, scalar2=None, op0=ALU.max)
                                pend.append((hd, r))
                    P.I('act', 'copy', out=score[:, c0:c0 + n], in_=SC[:, 0:n])

            def bisect(s):
                nkb = nkb_of(s)
                KE = nkb * 128
                MB = MBs[s % 2]
                P.I('dve', 'tensor_scalar', out=MB[:, 0:KE], in0=score[:, 0:KE], scalar1=1.0, scalar2=-1e30,
                    op0=ALU.mult, op1=ALU.max, accum_out=col['m'][:])
                P.I('dve', 'tensor_scalar', out=MB[:, 0:KE], in0=score[:, 0:KE], scalar1=-1.0, scalar2=-1e30,
                    op0=ALU.mult, op1=ALU.max, accum_out=col['mid'][:])
                P.I('dve', 'tensor_tensor', out=col['m'][:], in0=col['m'][:], in1=col['mid'][:], op=ALU.max)
                r0 = max(0, KE - 1024)
                rn = KE - r0
                P.I('dve', 'tensor_scalar', out=col['qrel'][:], in0=qpos[:, s:s + 1], scalar1=float(-r0), scalar2=None, op0=ALU.add)
                P.I('dve', 'tensor_scalar', out=pen[:, 0:rn], in0=kcol[:, 0:rn], scalar1=col['qrel'][:], scalar2=-1e30,
                    op0=ALU.is_gt, op1=ALU.mult)
                P.I('dve', 'tensor_tensor', out=score[:, r0:KE], in0=score[:, r0:KE], in1=pen[:, 0:rn], op=ALU.add)
                P.I('dve', 'memset', ap=score[:, 0:PADN], constant=-1e30)
                P.I('dve', 'tensor_scalar', out=col['wh'][:], in0=col['m'][:], scalar1=1.0, scalar2=2.0, op0=ALU.add, op1=ALU.mult)
                P.I('dve', 'tensor_scalar', out=tabA[:], in0=pw2[:], scalar1=col['wh'][:], scalar2=-0.25, op0=ALU.mult, op1=ALU.mult)
                P.I('dve', 'tensor_scalar', out=tabB[:], in0=pw2[:], scalar1=col['wh'][:], scalar2=0.5, op0=ALU.mult, op1=ALU.mult)
                P.I('dve', 'memset', ap=col['mid'][:], constant=0.0)
                for itn in range(NBIS):
                    P.I('dve', 'tensor_scalar', out=MB[:, 0:KE], in0=score[:, 0:KE], scalar1=col['mid'][:], scalar2=0.0,
                        op0=ALU.is_ge, op1=ALU.add, accum_out=col['cnt'][:])
                    P.I('dve', 'tensor_scalar', out=col['g'][:], in0=col['cnt'][:], scalar1=TOPK - 0.5, scalar2=tabB[:, itn:itn + 1],
                        op0=ALU.is_ge, op1=ALU.mult)
                    P.I('dve', 'scalar_tensor_tensor', out=col['mid'][:], in0=col['mid'][:], scalar=tabA[:, itn:itn + 1], in1=col['g'][:],
                        op0=ALU.add, op1=ALU.add)
                P.I('dve', 'scalar_tensor_tensor', out=col['lo'][:], in0=tabA[:, NBIS:NBIS + 1], scalar=2.0, in1=col['mid'][:],
                    op0=ALU.mult, op1=ALU.add)
                P.I('dve', 'tensor_scalar', out=MB[:, 0:KE], in0=score[:, 0:KE], scalar1=col['lo'][:], scalar2=NEG,
                    op0=ALU.is_lt, op1=ALU.mult)

            def attention(s):
                i = s % 2
                nkb = nkb_of(s)
                MB = MBs[s % 2]
                for g in range(2):
                    for sbk_ in range(nkb):
                        ks = slice(sbk_ * 128, sbk_ * 128 + 128)
                        pt = PT[cnt_['li'] % 4]
                        Lb = Lall[cnt_['li'] % 5]
                        cnt_['li'] += 1
                        P.I('pe', 'matmul', out=Lb[:], lhsT=kT[:, g, ks], rhs=qT[i][:, 4 * g:4 * g + 4, :], start=True, stop=False)
                        P.I('pe', 'matmul', out=Lb[:], lhsT=MB[:, ks], rhs=I4[:], start=False, stop=True)
                        P.I('act', 'activation', out=pt[:], in_=Lb[:], func=AF.Exp, scale=scale)
                        for hh in range(4):
                            P.I('pe', 'matmul', out=Osl[hh], lhsT=pt[:, hh * 128:(hh + 1) * 128], rhs=Vx[:, sbk_, g, :],
                                start=(sbk_ == 0 and hh in (0, 3)), stop=(sbk_ == nkb - 1), skip_group_check=(hh < 3))
                    for hh in range(4):
                        h_ = 4 * g + hh
                        P.I('dve', 'tensor_scalar', out=col['rinv'][:], in0=Osl[hh][:, 128:129], scalar1=1e-30, scalar2=None, op0=ALU.add)
                        P.I('dve', 'reciprocal', out=col['rinv'][:], in_=col['rinv'][:])
                        P.I('dve', 'scalar_tensor_tensor', out=att[:, h_ * 128:(h_ + 1) * 128], in0=Osl[hh][:, 0:128],
                            scalar=col['rinv'][:], in1=sgq[i][:, h_ * 128:(h_ + 1) * 128], op0=ALU.mult, op1=ALU.mult)
                ptt = OA.full().map(lambda a: a.bitcast(BF16).rearrange("p (h n) -> p h n", h=8))
                for hh in range(8):
                    P.I('pe', 'transpose', out=ptt[:, hh, :], in_=att[:, hh * 128:(hh + 1) * 128], identity=identb[:])
                aT = attT[s % 2]
                P.I('act', 'copy', out=aT[:], in_=ptt)
                P.dma('pool', out=io['attnT'][s].map(lambda a: a.rearrange("c p n -> p c n")), in_=aT[:])

            ldq(0)
            for s in range(NS + 1):
                if s < NS:
                    indexer(s)
                    bisect(s)
                if s >= 1:
                    attention(s - 1)
                if s + 1 < NS:
                    ldq(s + 1)
            P.barrier()


def build_p4(NB):
    P = Prog()
    NS = 1 + (NB - 1) // 8
    PP = NB * 128
    io = dict(
        mixo=P.din("mixo", [NS, 24, 128, 128], BF16),
        ssqo=P.din("ssqo", [NS, 8, 128], F32),
        attnT=P.din("attnT", [NS, 8, 128, 128], BF16),
        h=P.din("h", [NS, 128, D], F32),
        wout=P.din("wout", [32, 128, D], F32),
        postw=P.din("postw", [1, D], F32),
        rowvalid=P.din("rowvalid", [128, NS], F32),
        consts=P.din("consts", [128, 128], F32),
        hout=P.dout("hout", [NS, 128, D], F32),
    )
    emit_p4(P, NB, io)
    return P.finish()


def emit_p4(P, NB, io, NS):
    P.mark("p4")
    with contextlib.ExitStack() as st:
        sb = lambda n, s, d, dma=False: P.sb("p4_" + n, s, d, dma=dma, stack=st)
        wbf = sb("wbf", [128, 32, D], BF16)
        stg = [sb(f"stg{i}", [128, 1024], F32, True) for i in range(2)]
        postw = sb("postw", [128, D], F32, True)
        rowv = sb("rowv", [128, NS], F32, True)
        ident = sb("ident", [128, 128], F32, True)
        P.dma('sp', out=postw[:], in_=io['postw'].full().map(lambda a: a.broadcast_to([128, D])))
        P.dma('sp', out=rowv[:], in_=io['rowvalid'].full())
        P.dma('sp', out=ident[:], in_=io['consts'].full())
        for c2 in range(64):
            c, hf = c2 // 2, c2 % 2
            s = stg[c2 % 2]
            P.dma('sp', out=s[:], in_=io['wout'][c, :, hf * 1024:(hf + 1) * 1024])
            eng = ('dve', 'pool', 'act')[c2 % 3]
            if eng == 'act':
                P.I('act', 'copy', out=wbf[:, c, hf * 1024:(hf + 1) * 1024], in_=s[:])
            else:
                P.I(eng, 'tensor_copy', out=wbf[:, c, hf * 1024:(hf + 1) * 1024], in_=s[:])
        mt = [sb(f"mt{i}", [128, 32, 128], BF16, True) for i in range(2)]
        ht = [sb(f"ht{i}", [128, D], F32, True) for i in range(2)]
        sq8 = [sb(f"sq8{i}", [8, 128], F32, True) for i in range(2)]
        acc = sb("acc", [128, D], F32)
        junk = sb("junk", [128, D], BF16)
        hn = [sb("hn0", [128, D], F32)] * 2
        rsg = sb("rsg", [128, 4], F32)
        sqT = sb("sqT", [128, 8], F32)
        ss = sb("ss", [128, 1], F32)
        rs = sb("rs", [128, 1], F32)
        Po = P.ps("p4_Po", [128, 512], F32, stack=st)
        Pg = [P.ps(f"p4_Pg{i}", [128, 512], F32, stack=st) for i in range(4)]
        pq = P.ps("p4_pq", [128, 8], F32, stack=st)

        def ld(s):
            i = s % 2
            P.dma('sp', out=mt[i][:, 0:24, :], in_=io['mixo'](s))
            P.dma('sp', out=mt[i][:, 24:32, :], in_=io['attnT'][s].map(lambda a: a.rearrange("c p n -> p c n")))
            P.dma('sp', out=ht[i][:], in_=io['h'][s])
            P.dma('sp', out=sq8[i][:], in_=io['ssqo'](s))
        ld(0)
        for s in range(NS):
            if s + 1 < NS:
                ld(s + 1)
            i = s % 2
            m = mt[i]
            P.I('pe', 'transpose', out=pq[:], in_=sq8[i][:], identity=ident[0:8, 0:8])
            P.I('dve', 'tensor_copy', out=sqT[:], in_=pq[:])
            s3 = sqT[:].map(lambda a: a.rearrange("p (g t) -> p g t", t=2))
            P.I('dve', 'tensor_tensor', out=rsg[:], in0=s3[:, :, 0], in1=s3[:, :, 1], op=ALU.add)
            rsqrt_small(P, rsg[:], rsg[:], 1.0 / 512, EPS)
            for dtile in range(4):
                ds = slice(dtile * 512, dtile * 512 + 512)
                lst = [(3 * c, c) for c in range(8)] + [(24 + j, 24 + j) for j in range(8)]
                for n_, (mi, wi_) in enumerate(lst):
                    P.I('pe', 'matmul', out=Po[:], lhsT=m[:, mi, :], rhs=wbf[:, wi_, ds], start=(n_ == 0), stop=(n_ == len(lst) - 1))
                for g in range(4):
                    lst = [(3 * c + ci, 8 + 2 * c + (ci - 1)) for c in (2 * g, 2 * g + 1) for ci in (1, 2)]
                    for n_, (mi, wi_) in enumerate(lst):
                        P.I('pe', 'matmul', out=Pg[g][:], lhsT=m[:, mi, :], rhs=wbf[:, wi_, ds], start=(n_ == 0), stop=(n_ == len(lst) - 1))
                P.I('act', 'copy', out=acc[:, ds], in_=Po[:])
                for g in range(4):
                    P.I('dve', 'scalar_tensor_tensor', out=acc[:, ds], in0=Pg[g][:], scalar=rsg[:, g:g + 1], in1=acc[:, ds],
                        op0=ALU.mult, op1=ALU.add)
            P.I('act', 'activation', out=junk[:], in_=acc[:], func=AF.Square, accum_out=ss[:])
            rsqrt_small(P, rs[:], ss[:], 1.0 / D, EPS)
            o = hn[i]
            P.I('dve', 'scalar_tensor_tensor', out=o[:], in0=acc[:], scalar=rs[:], in1=postw[:], op0=ALU.mult, op1=ALU.mult)
            P.I('pool', 'tensor_tensor', out=o[:], in0=o[:], in1=ht[i][:], op=ALU.add)
            if s == 0:
                P.I('dve', 'tensor_scalar', out=o[:], in0=o[:], scalar1=rowv[:, 0:1], scalar2=None, op0=ALU.mult)
            P.dma('pool', out=io['hout'][s], in_=o[:])
        P.barrier()


def reg(buf, *idx):
    return Region(buf, buf._base()[idx] if idx else buf._base())


def build_fused(NB, depth, phases="1234"):
    P = Prog()
    PP = NB * 128
    h0 = P.din("h0", [NB, 128, D], F32)
    w_in = P.din("w_in", [depth, D, 10864], F32)
    w_out = P.din("w_out", [depth, 32, 128, D], F32)
    prew = P.din("prew", [depth, 128, NK], F32)
    postw = P.din("postw", [depth, 1, D], F32)
    poolw = P.din("poolw", [depth, 8, 256, 128], F32)
    pvec = P.din("pvec", [depth, 8, 128, 32], F32)
    convw = P.din("convw", [depth, 8, 128, 16], F32)
    rows = P.din("rows", [depth, 8, 128, 520], F32)
    consts2 = P.din("consts2", [128, 384], F32)
    ident = P.din("ident", [128, 128], F32)
    ropek = P.din("ropek", [NB, 128, 96], F32)
    ropeq = P.din("ropeq", [NB, 128, 512], F32)
    qpos = P.din("qpos", [128, NB], F32)
    kpos = P.din("kpos", [128, PP], F32)
    rowvalid = P.din("rowvalid", [128, NB], F32)
    hout = P.dout("hout", [NB, 128, D], F32)
    uT = P.dint("f_uT", [NB, NK, 128, 128], BF16)
    mixB = P.dint("f_mixB", [8, 3, 128, PP], BF16)
    ssq = P.dint("f_ssq", [8, NB, 128], F32)
    attnT = P.dint("f_attnT", [NB, 8, 128, 128], BF16)
    hmid = [P.dint("f_h%d" % i, [NB, 128, D], F32) for i in range(max(depth - 1, 0))]
    sbm = list(range(NB))
    for l in range(depth):
        hin = h0 if l == 0 else hmid[l - 1]
        ho = hout if l == depth - 1 else hmid[l]
        wl = reg(w_in, l)
        if '1' in phases:
            emit_p1(P, NB, hin, ident, uT)
        for c in range(NCORES):
            g = c // 2
            ranges = [(O_PV + g * 256, 256, 0), (O_PG + c * 128, 128, 256), (O_X + 256 * c, 256, 384),
                      (O_B + 128 * g, 128, 640), (O_C + 128 * g, 128, 768), (O_Z + 256 * c, 256, 896),
                      (O_DT + 4 * c, 4, 1152)]
            io = dict(uT=uT, wld=ranged_loader(wl, ranges), prew=reg(prew, l), poolw=reg(poolw, l, c), pvec=reg(pvec, l, c),
                      convw=reg(convw, l, c), rows=reg(rows, l, c), consts=consts2, mixB=reg(mixB, c), ssq=reg(ssq, c))
            if '2' in phases:
                emit_p2(P, NB, io)
        io = dict(uT=uT, uTo=uT, prew=reg(prew, l), ropek=ropek, ropeq=ropeq, qpos=qpos, kpos=kpos, consts=ident, attnT=attnT,
                  wldA=ranged_loader(wl, [(O_Q, 1024, 0), (O_G, 1024, 1024), (O_IQ, 1024, 2048), (O_IW, 16, 3072)]),
                  wldKV=ranged_loader(wl, [(O_K, 256, 0), (O_V, 256, 256), (O_IK, 64, 512), (O_IK, 64, 576)]))
        if '3' in phases:
            emit_p3(P, NB, io, NB, sbm, tag="_%d" % l)
        io = dict(mixo=lambda s_: mixB[:, :, :, s_ * 128:(s_ + 1) * 128].map(lambda a: a.rearrange("v c p n -> p (v c) n")),
                  ssqo=lambda s_: ssq[:, s_, :], attnT=attnT, h=hin, wout=reg(w_out, l), postw=reg(postw, l),
                  rowvalid=rowvalid, consts=ident, hout=ho)
        if '4' in phases:
            emit_p4(P, NB, io, NB)
    P.mark("end")
    global LAST_MARKS
    LAST_MARKS = P.marks
    print("fused program: ecnt", P.ecnt, "nwait", P.nwait, flush=True)
    return P.finish()


_PROGS = {}


def rope_tables(NB):
    PP = NB * 128
    pos = np.maximum(np.arange(PP) - PADN, 0).astype(np.float32)

    def cs(rot):
        half = rot // 2
        inv = np.power(np.float32(500000.0), -(np.arange(half, dtype=np.float32) * 2.0 / rot)).astype(np.float32)
        ang = pos[:, None] * inv[None, :]
        return np.cos(ang).astype(np.float32), np.sin(ang).astype(np.float32)
    c32, s32 = cs(32)
    c16, s16 = cs(16)
    ropek = np.concatenate([np.tile(c32, (1, 2)), np.tile(s32, (1, 2)), np.tile(c16, (1, 2)), np.tile(s16, (1, 2))], axis=1)
    ropeq = np.concatenate([np.tile(c32, (1, 8)), np.tile(s32, (1, 8)), np.tile(c16, (1, 16)), np.tile(s16, (1, 16))], axis=1)
    return ropek.reshape(NB, 128, 96), ropeq.reshape(NB, 128, 512)


def consts_p2():
    ident = np.eye(128, dtype=np.float32)
    s = np.arange(128)
    U = (s[:, None] <= s[None, :]).astype(np.float32)
    MBT = np.where(s[None, :] >= s[:, None], 0.0, NEG).astype(np.float32)
    return np.concatenate([ident, U, MBT], axis=1)


def run(nc, maps):
    res = run_bass_kernel_spmd(nc, maps, core_ids=list(range(len(maps))))
    return res.results


def small_params(inp, depth):
    windows = (2, 4, 8, 16)
    poolw = np.zeros((depth, 8, 256, 128), np.float32)
    pvec = np.zeros((depth, 8, 128, 32), np.float32)
    convw = np.zeros((depth, 8, 128, 16), np.float32)
    rows = np.zeros((depth, 8, 128, 520), np.float32)
    for l in range(depth):
        for c in range(8):
            g, half = c // 2, c % 2
            poolw[l, c] = inp['pool_w'][l][g][:, half * 128:(half + 1) * 128]
            pvec[l, c, :, 0] = inp['pool_scale'][l][c * 128:(c + 1) * 128]
            w = windows[g]
            pvec[l, c, :, 1 + g] = np.float32(1.0) / np.float32(w)
            j = np.arange(16)
            pvec[l, c, :, 5:21] = (np.float32(w) / np.minimum(w, j + 1).astype(np.float32))[None, :]
            ccols = [np.arange(256 * c, 256 * c + 128), np.arange(256 * c + 128, 256 * c + 256),
                     np.arange(2048 + 128 * g, 2048 + 128 * (g + 1)), np.arange(2560 + 128 * g, 2560 + 128 * (g + 1))]
            for ci, cc in enumerate(ccols):
                pvec[l, c, :, 21 + ci] = inp['conv_b'][l][cc]
                convw[l, c, :, ci * 4:(ci + 1) * 4] = inp['conv_w'][l][:, cc].T
            rows[l, c, :, 0:4] = inp['dt_bias'][l][4 * c:4 * c + 4][None]
            rows[l, c, :, 4:8] = inp['a_log'][l][4 * c:4 * c + 4][None]
            rows[l, c, :, 8:264] = np.repeat(inp['d_skip'][l][4 * c:4 * c + 4], 64)[None]
            rows[l, c, :, 264:520] = inp['ssd_norm_w'][l][256 * c:256 * (c + 1)][None]
    return poolw, pvec, convw, rows


def forward(inp, NB, depth, debug=None, ncores=NCORES):
    PP = NB * 128
    x = np.asarray(inp['x'], np.float32)[0]
    h0 = np.zeros((PP, D), np.float32)
    h0[PADN:PADN + NMETA] = inp['meta_tokens']
    h0[128:] = x
    ropek, ropeq = rope_tables(NB)
    poolw, pvec, convw, rows = small_params(inp, depth)
    rowvalid = np.ones((128, NB), np.float32)
    rowvalid[:PADN, 0] = 0.0
    m = dict(
        h0=h0.reshape(NB, 128, D),
        w_in=np.ascontiguousarray(inp['w_in'][:depth], np.float32),
        w_out=np.ascontiguousarray(inp['w_out'][:depth].reshape(depth, 32, 128, D), np.float32),
        prew=np.ascontiguousarray(inp['pre_norm_w'][:depth].reshape(depth, NK, 128).transpose(0, 2, 1)),
        postw=np.ascontiguousarray(inp['post_norm_w'][:depth].reshape(depth, 1, D)),
        poolw=poolw, pvec=pvec, convw=convw, rows=rows,
        consts2=consts_p2(), ident=np.eye(128, dtype=np.float32), ropek=ropek, ropeq=ropeq,
        qpos=np.ascontiguousarray((np.arange(128, dtype=np.float32)[:, None] + 128.0 * np.arange(NB, dtype=np.float32)[None, :])),
        kpos=np.ascontiguousarray(np.broadcast_to(np.arange(PP, dtype=np.float32)[None], (128, PP))),
        rowvalid=rowvalid)
    key = (NB, depth)
    if key not in _PROGS:
        _PROGS[key] = build_fused(NB, depth)
    res = run(_PROGS[key], [m] * ncores)
    hout = np.asarray(res[0]['hout'])
    return hout.reshape(PP, D)[128:][None]


def kernel(**inputs):
    inp = {k: np.asarray(v) for k, v in inputs.items()}
    S = inp['x'].shape[1]
    NB = (S + 128) // 128
    depth = inp['w_in'].shape[0]
    return forward(inp, NB, depth).astype(np.float32)
```

```python
import contextlib
import numpy as np
import ml_dtypes
import concourse.bass as bass
import concourse.mybir as mybir
from concourse.bass_utils import run_bass_kernel_spmd

F32 = mybir.dt.float32
BF16 = mybir.dt.bfloat16
AF = mybir.ActivationFunctionType
ALU = mybir.AluOpType
AX = mybir.AxisListType

D = 2048
NK = 16
PADN = 112
NMETA = 16
EPS = 1e-6
NEG = -30000.0
NBIS = 18
TOPK = 256
NCORES = 8
SAME_ENGINE_SYNC = True

O_PV, O_PG, O_Z, O_X, O_B, O_C, O_DT = 0, 1024, 2048, 4096, 6144, 6656, 7168
O_Q, O_K, O_V, O_G, O_IQ, O_IK, O_IW = 7200, 8224, 8480, 8736, 9760, 10784, 10848


class View:
    def __init__(self, buf, ap):
        self.buf = buf
        self.ap = ap

    def map(self, f):
        return View(self.buf, f(self.ap))

    def __getitem__(self, idx):
        return View(self.buf, self.ap[idx])


class Buf:
    def __init__(self, t, name, dma_sem=None, ap0=None):
        self.t = t
        self.ap0 = ap0
        self.name = name
        self.w = []
        self.r = []
        self.dma_sem = dma_sem
        self.dma_cnt = 0
        self.psum = False

    def _base(self):
        return self.ap0 if self.ap0 is not None else self.t.ap()

    def __getitem__(self, idx):
        return View(self, self._base()[idx])

    def full(self):
        return View(self, self._base())

    def sub(self, name, idx):
        return Region(self, self._base()[idx])


class Region:
    def __init__(self, buf, ap0):
        self.buf = buf
        self.ap0 = ap0

    def __getitem__(self, idx):
        return View(self.buf, self.ap0[idx])

    def full(self):
        return View(self.buf, self.ap0)


class Prog:
    ENG = ('pe', 'dve', 'act', 'pool', 'sp')

    def __init__(self):
        self.nc = bass.Bass("TRN2", target_bir_lowering=False)
        nc = self.nc
        self.engs = {'pe': nc.tensor, 'dve': nc.vector, 'act': nc.scalar, 'pool': nc.gpsimd, 'sp': nc.sync}
        self.esem = {}
        self.ecnt = {}
        self.seen = {k: {} for k in self.engs}
        for k in ('pe', 'dve', 'act', 'pool'):
            self.esem[k] = nc.alloc_semaphore("e_" + k)
            self.ecnt[k] = 0
        self.st = contextlib.ExitStack()
        self.outs = []
        self.nwait = 0
        self.dmabufs = []
        self.sempool = []
        self.rec = None

    def _dsem(self, name, fresh=False):
        if self.sempool and not fresh:
            return self.sempool.pop()
        self.nsem = getattr(self, 'nsem', 0) + 1
        return (self.nc.alloc_semaphore("d_%d" % self.nsem), 0)

    def mark(self, label):
        if not hasattr(self, 'marks'):
            self.marks = []
        self.marks.append((label, dict(self.ecnt)))

    def barrier(self):
        deps = [(self.esem[k], self.ecnt[k]) for k in self.esem if self.ecnt[k] > 0]
        deps += [(b.dma_sem, b.dma_cnt) for b in self.dmabufs if b.dma_cnt > 0]
        for e in self.ENG:
            self._wait(e, deps)

    def sb(self, name, shape, dt, dma=False, stack=None):
        self.uid = getattr(self, 'uid', 0) + 1
        name = "%s_%d" % (name, self.uid)
        t = (stack or self.st).enter_context(self.nc.sbuf_tensor(name, list(shape), dt))
        b = Buf(t, name)
        if dma:
            b.dma_sem, b.dma_cnt = self._dsem(name)
            self.dmabufs.append(b)
            if stack is not None:
                def _rel(b=b):
                    self.sempool.append((b.dma_sem, b.dma_cnt))
                    self.dmabufs.remove(b)
                stack.callback(_rel)
        return b

    def ps(self, name, shape, dt, stack=None):
        self.uid = getattr(self, 'uid', 0) + 1
        name = "%s_%d" % (name, self.uid)
        t = (stack or self.st).enter_context(self.nc.psum_tensor(name, list(shape), dt))
        b = Buf(t, name)
        b.psum = True
        return b

    def din(self, name, shape, dt):
        return Buf(self.nc.dram_tensor(name, list(shape), dt, kind="ExternalInput"), name)

    def dout(self, name, shape, dt):
        b = Buf(self.nc.dram_tensor(name, list(shape), dt, kind="ExternalOutput"), name)
        b.dma_sem, b.dma_cnt = self._dsem(name, fresh=True)
        self.outs.append(b)
        self.dmabufs.append(b)
        return b

    def dint(self, name, shape, dt):
        b = Buf(self.nc.dram_tensor(name, list(shape), dt, kind="Internal"), name)
        b.dma_sem, b.dma_cnt = self._dsem(name, fresh=True)
        self.dmabufs.append(b)
        return b

    def _wait(self, ename, deps):
        eng = self.engs[ename]
        best = {}
        for (s, v) in deps:
            key = id(s)
            if key not in best or best[key][1] < v:
                best[key] = (s, v)
        for key, (s, v) in best.items():
            if ename == 'pe' and s is self.esem['pe']:
                continue
            if not SAME_ENGINE_SYNC and ename in self.esem and s is self.esem[ename]:
                continue
            if self.seen[ename].get(key, 0) >= v:
                continue
            eng.wait_ge(s, v)
            self.nwait += 1
            self.seen[ename][key] = v

    @staticmethod
    def _deps(reads, writes):
        deps = []
        for b in reads:
            deps += b.w
        for b in writes:
            deps += b.w + b.r
        return deps

    def I(self, ename, method, **kw):
        if self.rec is not None:
            self.rec.append(('I', ename, method, kw))
            return None
        reads, writes = [], []
        args = {}
        for k, v in kw.items():
            if isinstance(v, View):
                (writes if (k in ('out', 'accum_out', 'ap') or v.buf.psum) else reads).append(v.buf)
                args[k] = v.ap
            else:
                args[k] = v
        self._wait(ename, self._deps(reads, writes))
        inst = getattr(self.engs[ename], method)(**args)
        s = self.esem[ename]
        inst.then_inc(s, 1)
        self.ecnt[ename] += 1
        tok = (s, self.ecnt[ename])
        for b in reads:
            if b in writes:
                continue
            b.r = [t for t in b.r if t[0] is not s] + [tok]
        for b in writes:
            b.w = [tok]
            b.r = []
        return inst

    def replay(self, item):
        rec, self.rec = self.rec, None
        if item[0] == 'I':
            self.I(item[1], item[2], **item[3])
        else:
            self.dma(item[1], out=item[2], in_=item[3], **item[4])
        self.rec = rec

    def replay_merged(self, a, b):
        na, nb_ = len(a), len(b)
        j = 0
        for i_, it in enumerate(a):
            self.replay(it)
            tgt = ((i_ + 1) * nb_) // max(na, 1)
            while j < tgt:
                self.replay(b[j])
                j += 1
        while j < nb_:
            self.replay(b[j])
            j += 1

    def dma(self, q, out, in_, **kw):
        if self.rec is not None:
            self.rec.append(('D', q, out, in_, kw))
            return None
        dst, src = out.buf, in_.buf
        assert dst.dma_sem is not None, dst.name
        self._wait(q, self._deps([src], [dst]))
        inst = self.engs[q].dma_start(out=out.ap, in_=in_.ap, **kw)
        inst.then_inc(dst.dma_sem, 16)
        dst.dma_cnt += 16
        tok = (dst.dma_sem, dst.dma_cnt)
        src.r = [t for t in src.r if t[0] is not dst.dma_sem] + [tok]
        dst.w = [tok]
        dst.r = []
        return inst

    def finish(self):
        deps = []
        for b in self.outs:
            deps += b.w
        self._wait('sp', deps)
        self.st.close()
        return self.nc


def rsqrt_small(P, out, in_, scale, eps):
    P.I('dve', 'tensor_scalar', out=out, in0=in_, scalar1=scale, scalar2=eps, op0=ALU.mult, op1=ALU.add)
    P.I('act', 'activation', out=out, in_=out, func=AF.Sqrt)
    P.I('dve', 'reciprocal', out=out, in_=out)


def tiles_of(nb_total, step=4):
    out = []
    b = 0
    while b < nb_total:
        n = min(step, nb_total - b)
        out.append((b, n))
        b += n
    return out


def load_weights_bf16(P, wd, ncols, wbf, prew, stg, col0=0, colchunk=None):
    colchunk = colchunk or ncols
    i = 0
    for k in range(NK):
        for c0 in range(0, ncols, colchunk):
            cn = min(colchunk, ncols - c0)
            s = stg[i % len(stg)]
            P.dma('sp', out=s[:, 0:cn], in_=wd[k * 128:(k + 1) * 128, c0:c0 + cn])
            eng = 'dve' if i % 2 == 0 else 'pool'
            P.I(eng, 'tensor_scalar', out=wbf[:, k, col0 + c0:col0 + c0 + cn], in0=s[:, 0:cn],
                scalar1=prew[:, k:k + 1], scalar2=None, op0=ALU.mult)
            i += 1


def ranged_loader(wfull, ranges):
    def wld(P, wbf, prew, stg):
        i = 0
        width = stg[0].full().ap.shape[1]
        for k in range(NK):
            for (sc, n, dc) in ranges:
                for c0 in range(0, n, width):
                    cn = min(width, n - c0)
                    s_ = stg[i % len(stg)]
                    P.dma('sp', out=s_[:, 0:cn], in_=wfull[k * 128:(k + 1) * 128, sc + c0:sc + c0 + cn])
                    eng = 'dve' if i % 2 == 0 else 'pool'
                    P.I(eng, 'tensor_scalar', out=wbf[:, k, dc + c0:dc + c0 + cn], in0=s_[:, 0:cn],
                        scalar1=prew[:, k:k + 1], scalar2=None, op0=ALU.mult)
                    i += 1
    return wld


def load_ut_tile(P, ut, src, b0, nb, q='sp'):
    for k0 in range(0, NK, 4):
        for bb in range(nb):
            P.dma(q, out=ut[:, k0:k0 + 4, bb * 128:(bb + 1) * 128],
                  in_=src[b0 + bb, k0:k0 + 4].map(lambda a: a.rearrange("k d n -> d k n")))


def build_p1(NS):
    P = Prog()
    h = P.din("h", [NS, 128, D], F32)
    identd = P.din("ident", [128, 128], F32)
    uT = P.dout("uT", [NS, NK, 128, 128], BF16)
    emit_p1(P, NS, h, identd, uT)
    return P.finish()


def emit_p1(P, NS, h, identd, uT):
    with contextlib.ExitStack() as st:
        ident = P.sb("p1_ident", [128, 128], F32, dma=True, stack=st)
        identb = P.sb("p1_identb", [128, 128], BF16, stack=st)
        xt = [P.sb(f"p1_xt{i}", [128, D], F32, dma=True, stack=st) for i in range(2)]
        sq = P.sb("p1_sq", [128, D], BF16, stack=st)
        ss = P.sb("p1_ss", [128, 1], F32, stack=st)
        rs = P.sb("p1_rs", [128, 1], F32, stack=st)
        ub = [P.sb(f"p1_ub{i}", [128, D], BF16, stack=st) for i in range(2)]
        uTs = [P.sb(f"p1_uT{i}", [128, NK, 128], BF16, stack=st) for i in range(2)]
        pt = [P.ps(f"p1_pt{i}", [128, 4, 128], BF16, stack=st) for i in range(2)]
        P.dma('sp', out=ident[:], in_=identd.full())
        P.I('dve', 'tensor_copy', out=identb[:], in_=ident[:])
        for s in range(NS):
            xb = xt[s % 2]
            P.dma('sp', out=xb[:], in_=h[s])
            P.I('act', 'activation', out=sq[:], in_=xb[:], func=AF.Square, accum_out=ss[:])
            rsqrt_small(P, rs[:], ss[:], 1.0 / D, EPS)
            u = ub[s % 2]
            P.I('dve', 'tensor_scalar', out=u[:], in0=xb[:], scalar1=rs[:], scalar2=None, op0=ALU.mult)
            ut = uTs[s % 2]
            for k4 in range(NK // 4):
                p = pt[k4 % 2]
                for kk in range(4):
                    k = k4 * 4 + kk
                    P.I('pe', 'transpose', out=p[:, kk, :], in_=u[:, k * 128:(k + 1) * 128], identity=identb[:])
                if k4 % 2:
                    P.I('act', 'copy', out=ut[:, k4 * 4:(k4 + 1) * 4, :], in_=p[:])
                else:
                    P.I('dve', 'tensor_copy', out=ut[:, k4 * 4:(k4 + 1) * 4, :], in_=p[:])
            P.dma('pool', out=uT[s].map(lambda a: a.rearrange("k d n -> d k n")), in_=ut[:])
        P.barrier()


NCB = 1156


def build_p2(NB):
    P = Prog()
    PP = NB * 128
    io = dict(
        uT=P.din("uT", [NB, NK, 128, 128], BF16),
        wB=P.din("wB", [D, NCB], F32),
        prew=P.din("prew", [128, NK], F32),
        poolw=P.din("poolw", [256, 128], F32),
        pvec=P.din("pvec", [128, 32], F32),
        convw=P.din("convw", [128, 16], F32),
        rows=P.din("rows", [128, 520], F32),
        consts=P.din("consts", [128, 384], F32),
        mixB=P.dout("mixB", [3, 128, PP], BF16),
        ssq=P.dout("ssq", [NB, 128], F32),
    )
    emit_p2(P, NB, io)
    return P.finish()


def emit_p2(P, NB, io):
    P.mark("p2")
    PP = NB * 128
    uT, mixB, ssqd = io['uT'], io['mixB'], io['ssq']
    with contextlib.ExitStack() as st:
        sb = lambda n, s, d, dma=False: P.sb("p2_" + n, s, d, dma=dma, stack=st)
        ps = lambda n, s, d: P.ps("p2_" + n, s, d, stack=st)
        prew = sb("prew", [128, NK], F32, True)
        pvec = sb("pvec", [128, 32], F32, True)
        convw = sb("convw", [128, 16], F32, True)
        rows = sb("rows", [128, 520], F32, True)
        consts = sb("consts", [128, 384], F32, True)
        poolw_s = sb("poolw_s", [128, 2, 128], F32, True)
        poolw = sb("poolw", [128, 2, 128], BF16)
        identb = sb("identb", [128, 128], BF16)
        ones = sb("ones", [128, 128], F32)
        Aneg = sb("Aneg", [128, 4], F32)
        wbf = sb("wbf", [128, NK, NCB], BF16)
        stg = [sb(f"stg{i}", [128, NCB], F32, True) for i in range(2)]
        for t_, d_ in ((prew, io['prew']), (pvec, io['pvec']), (convw, io['convw']), (rows, io['rows']),
                       (consts, io['consts'])):
            P.dma('sp', out=t_[:], in_=d_.full())
        P.dma('sp', out=poolw_s[:], in_=io['poolw'].full().map(lambda a: a.rearrange("(c p) d -> p c d", p=128)))
        ident = consts[:, 0:128]
        U = consts[:, 128:256]
        MBT = consts[:, 256:384]
        P.I('dve', 'tensor_copy', out=poolw[:], in_=poolw_s[:])
        P.I('dve', 'tensor_copy', out=identb[:], in_=ident)
        P.I('dve', 'memset', ap=ones[:], constant=1.0)
        P.I('act', 'activation', out=Aneg[:], in_=rows[:, 4:8], func=AF.Exp)
        P.I('dve', 'tensor_scalar', out=Aneg[:], in0=Aneg[:], scalar1=-1.0, scalar2=None, op0=ALU.mult)
        io['wld'](P, wbf, prew, stg)
        dtb = rows[:, 0:4]
        dsk = rows[:, 8:264]
        nw = rows[:, 264:520]

        ut = [sb(f"ut{i}", [128, NK, 512], BF16, True) for i in range(3)]
        vraw = sb("vraw", [128, 2, 528], F32)
        xraw = sb("xraw", [128, 4, 515], F32)
        s2 = sb("s2", [128, 2, 528], F32)
        s4 = sb("s4", [128, 2, 528], F32)
        s8 = sb("s8", [128, 2, 528], F32)
        s16 = sb("s16", [128, 2, 528], F32)
        pacc = sb("pacc", [128, 2, 512], F32)
        ybf = sb("ybf", [128, 2, 512], BF16)
        sg = sb("sg", [128, 512], F32)
        cv = sb("cv", [128, 512], F32)
        xs_l = [sb(f"xs{i}", [128, 2, 512], F32) for i in range(2)]
        Bs_l = [sb(f"Bs{i}", [128, 512], BF16) for i in range(2)]
        Cs_l = [sb(f"Cs{i}", [128, 512], BF16) for i in range(2)]
        mixt = [sb(f"mixt{i}", [128, 3, 512], BF16) for i in range(2)]
        ssqc = sb("ssqc", [128, 128], F32)
        ssqT = sb("ssqT", [128, 128], F32)
        szs_l = [[sb(f"szs{i}{j}", [128, 256], F32) for j in range(4)] for i in range(2)]
        dt_l = [[sb(f"dt{i}{j}", [128, 4], F32) for j in range(4)] for i in range(2)]
        da_l = [[sb(f"da{i}{j}", [128, 4], F32) for j in range(4)] for i in range(2)]
        dab_l = [[sb(f"dab{i}{j}", [128, 4, 128], F32) for j in range(4)] for i in range(2)]
        negcs = sb("negcs", [128, 4], F32)
        ncsh = [sb(f"ncsh{i}", [128, 1], F32) for i in range(4)]
        ssq1 = sb("ssq1", [128, 1], F32)
        cl = sb("cl", [128, 4], F32)
        dte = sb("dte", [128, 4], F32)
        w2 = sb("w2", [128, 4], F32)
        ecs = sb("ecs", [128, 4], F32)
        cdec = sb("cdec", [128, 4], F32)
        CBTs = sb("CBTs", [128, 128], F32)
        LT = sb("LT", [128, 4, 128], F32)
        MT = sb("MT", [128, 4, 128], BF16)
        xdt = sb("xdt", [128, 256], BF16)
        xdtd = sb("xdtd", [128, 256], BF16)
        Bpm = sb("Bpm", [128, 128], BF16)
        ysb = sb("ysb", [128, 256], F32)
        ytmp = sb("ytmp", [128, 256], F32)
        ysq = sb("ysq", [128, 256], F32)
        ynb = sb("ynb", [128, 256], BF16)
        hT = sb("hT", [128, 256], F32)
        prevb = sb("prevb", [128, 256], BF16)

        pc = [ps(f"pc{i}", [128, 512], F32) for i in range(2)]
        pz = ps("pz", [128, 512], F32)
        csb = ps("csb", [128, 4, 128], F32)
        bkA = ps("bkA", [128, 512], F32)
        xpm = bkA.sub("xpm", (slice(None), slice(0, 256)))
        CBT = bkA.sub("CBT", (slice(None), slice(256, 384)))
        cspm = bkA.sub("cspm", (slice(None), slice(384, 388)))
        bkB = ps("bkB", [128, 512], F32)
        Yp = bkB.sub("Yp", (slice(None), slice(0, 256)))
        Yoff = bkB.sub("Yoff", (slice(None), slice(256, 512)))
        Stp = ps("Stp", [128, 256], F32)
        ptr = ps("ptr", [128, 3, 128], BF16)

        P.I('dve', 'memset', ap=vraw[:], constant=0.0)
        P.I('dve', 'memset', ap=xraw[:], constant=0.0)
        P.I('dve', 'memset', ap=hT[:], constant=0.0)
        P.I('dve', 'memset', ap=prevb[:], constant=0.0)
        P.I('dve', 'memset', ap=ssqc[:], constant=0.0)

        tl = tiles_of(NB)
        load_ut_tile(P, ut[0], uT, tl[0][0], tl[0][1])
        pci = 0
        recT, recC = [], []
        for ti, (b0, nb) in enumerate(tl):
            P.rec = []
            N = nb * 128
            u = ut[ti % 3]
            xs, Bs, Cs = xs_l[ti % 2], Bs_l[ti % 2], Cs_l[ti % 2]
            if ti + 1 < len(tl):
                load_ut_tile(P, ut[(ti + 1) % 3], uT, tl[ti + 1][0], tl[ti + 1][1])
            mx = mixt[ti % 2]

            def cproj(col0):
                nonlocal pci
                p = pc[pci % 2]
                pci += 1
                for k in range(NK):
                    P.I('pe', 'matmul', out=p[:, 0:N], lhsT=wbf[:, k, col0:col0 + 128], rhs=u[:, k, 0:N],
                        start=(k == 0), stop=(k == NK - 1))
                return p
            for cc in range(2):
                p = cproj(cc * 128)
                P.I('act', 'copy', out=vraw[:, cc, 16:16 + N], in_=p[:, 0:N])
            E = 16 + N
            P.I('pool', 'tensor_tensor', out=s2[:, :, 1:E], in0=vraw[:, :, 1:E], in1=vraw[:, :, 0:E - 1], op=ALU.add)
            P.I('pool', 'tensor_tensor', out=s4[:, :, 3:E], in0=s2[:, :, 3:E], in1=s2[:, :, 1:E - 2], op=ALU.add)
            P.I('pool', 'tensor_tensor', out=s8[:, :, 7:E], in0=s4[:, :, 7:E], in1=s4[:, :, 3:E - 4], op=ALU.add)
            P.I('pool', 'tensor_tensor', out=s16[:, :, 15:E], in0=s8[:, :, 15:E], in1=s8[:, :, 7:E - 8], op=ALU.add)
            P.I('dve', 'tensor_scalar', out=pacc[:, :, 0:N], in0=s2[:, :, 16:E], scalar1=pvec[:, 1:2], scalar2=None,
                op0=ALU.mult)
            for wi_, sw in ((1, s4), (2, s8), (3, s16)):
                P.I('dve', 'scalar_tensor_tensor', out=pacc[:, :, 0:N], in0=sw[:, :, 16:E],
                    scalar=pvec[:, 1 + wi_:2 + wi_], in1=pacc[:, :, 0:N], op0=ALU.mult, op1=ALU.add)
            if ti == 0:
                P.I('dve', 'tensor_tensor', out=pacc[:, :, PADN:128], in0=pacc[:, :, PADN:128],
                    in1=pvec[:, 5:21].map(lambda a: a.unsqueeze(1).broadcast_to([128, 2, 16])), op=ALU.mult)
            P.I('dve', 'tensor_tensor', out=ybf[:, :, 0:N], in0=pacc[:, :, 0:N], in1=vraw[:, :, 16:E], op=ALU.subtract)
            P.I('pool', 'tensor_copy', out=vraw[:, :, 0:16], in_=vraw[:, :, N:N + 16])
            p = cproj(256)
            P.I('act', 'activation', out=sg[:, 0:N], in_=p[:, 0:N], func=AF.Silu)
            p = pc[pci % 2]
            pci += 1
            for cc in range(2):
                P.I('pe', 'matmul', out=p[:, 0:N], lhsT=poolw[:, cc, :], rhs=ybf[:, cc, 0:N], start=(cc == 0),
                    stop=(cc == 1))
            P.I('dve', 'scalar_tensor_tensor', out=mx[:, 0, 0:N], in0=p[:, 0:N], scalar=pvec[:, 0:1], in1=sg[:, 0:N],
                op0=ALU.mult, op1=ALU.mult)
            for ci in range(4):
                p = cproj(384 + ci * 128)
                P.I('act', 'copy', out=xraw[:, ci, 3:3 + N], in_=p[:, 0:N])
                P.I('dve', 'tensor_scalar', out=cv[:, 0:N], in0=xraw[:, ci, 3:3 + N], scalar1=convw[:, ci * 4 + 3:ci * 4 + 4],
                    scalar2=pvec[:, 21 + ci:22 + ci], op0=ALU.mult, op1=ALU.add)
                for tp in range(3):
                    P.I('dve', 'scalar_tensor_tensor', out=cv[:, 0:N], in0=xraw[:, ci, tp:tp + N],
                        scalar=convw[:, ci * 4 + tp:ci * 4 + tp + 1], in1=cv[:, 0:N], op0=ALU.mult, op1=ALU.add)
                dst = xs[:, ci, 0:N] if ci < 2 else (Bs[:, 0:N] if ci == 2 else Cs[:, 0:N])
                P.I('act', 'activation', out=dst, in_=cv[:, 0:N], func=AF.Silu)
                P.I('pool', 'tensor_copy', out=xraw[:, ci, 0:3], in_=xraw[:, ci, N:N + 3])
            for bb in range(nb):
                blk = b0 + bb
                c0 = bb * 128
                cs_ = slice(c0, c0 + 128)
                szs, dt, da, dab = szs_l[ti % 2][bb], dt_l[ti % 2][bb], da_l[ti % 2][bb], dab_l[ti % 2][bb]
                for k in range(NK):
                    P.I('pe', 'matmul', out=pz[:, 0:260], lhsT=u[:, k, cs_], rhs=wbf[:, k, 896:1156],
                        start=(k == 0), stop=(k == NK - 1))
                P.I('act', 'activation', out=szs[:], in_=pz[:, 0:256], func=AF.Silu)
                P.I('dve', 'tensor_tensor', out=dt[:], in0=pz[:, 256:260], in1=dtb, op=ALU.add)
                P.I('act', 'activation', out=dt[:], in_=dt[:], func=AF.Exp)
                P.I('act', 'activation', out=dt[:], in_=dt[:], func=AF.Ln, bias=1.0)
                if blk == 0:
                    P.I('dve', 'memset', ap=dt[0:PADN, :], constant=0.0)
                P.I('dve', 'tensor_tensor', out=da[:], in0=dt[:], in1=Aneg[:], op=ALU.mult)
                P.I('pool', 'tensor_tensor', out=dab[:],
                    in0=ones[:].map(lambda a: a.unsqueeze(1).broadcast_to([128, 4, 128])),
                    in1=da[:].map(lambda a: a.unsqueeze(2).broadcast_to([128, 4, 128])), op=ALU.mult)
            recT.append(P.rec)
            P.rec = []
            for bb in range(nb):
                blk = b0 + bb
                c0 = bb * 128
                cs_ = slice(c0, c0 + 128)
                szs, dt, da, dab = szs_l[ti % 2][bb], dt_l[ti % 2][bb], da_l[ti % 2][bb], dab_l[ti % 2][bb]
                for cc in range(2):
                    P.I('pe', 'transpose', out=xpm[:, cc * 128:(cc + 1) * 128], in_=xs[:, cc, cs_], identity=ident)
                P.I('pe', 'transpose', out=ptr[:, 0, :], in_=Bs[:, cs_], identity=identb[:])
                P.I('act', 'copy', out=Bpm[:], in_=ptr[:, 0, :])
                P.I('pe', 'matmul', out=CBT[:], lhsT=Bs[:, cs_], rhs=Cs[:, cs_], start=True, stop=True)
                P.I('act', 'copy', out=CBTs[:], in_=CBT[:])
                P.I('pe', 'matmul', out=cspm[:], lhsT=U, rhs=da[:], start=True, stop=True)
                for hh in range(4):
                    P.I('pe', 'matmul', out=csb[:, hh, :], lhsT=dab[:, hh, :], rhs=U, start=True, stop=False)
                    P.I('pe', 'matmul', out=csb[:, hh, :], lhsT=ident, rhs=MBT, start=False, stop=True)
                P.I('dve', 'tensor_scalar', out=negcs[:], in0=cspm[:], scalar1=-1.0, scalar2=None, op0=ALU.mult)
                P.I('dve', 'tensor_copy', out=cl[:], in_=csb[:, :, 127])
                P.I('dve', 'tensor_tensor', out=dte[:], in0=negcs[:], in1=cl[:], op=ALU.add)
                P.I('act', 'activation', out=dte[:], in_=dte[:], func=AF.Exp)
                P.I('act', 'activation', out=ecs[:], in_=negcs[:], func=AF.Exp, scale=-1.0)
                P.I('act', 'activation', out=cdec[:], in_=cl[:], func=AF.Exp)
                for hh in range(4):
                    P.I('dve', 'tensor_scalar', out=ncsh[hh][:], in0=cspm[:, hh:hh + 1], scalar1=-1.0, scalar2=None, op0=ALU.mult)
                    P.I('act', 'activation', out=LT[:, hh, :], in_=csb[:, hh, :], func=AF.Exp, bias=ncsh[hh][:])
                P.I('pool', 'tensor_tensor', out=MT[:], in0=LT[:],
                    in1=CBTs[:].map(lambda a: a.unsqueeze(1).broadcast_to([128, 4, 128])), op=ALU.mult)
                P.I('dve', 'tensor_tensor', out=w2[:], in0=dt[:], in1=dte[:], op=ALU.mult)
                x3 = xpm[:].map(lambda a: a.rearrange("p (h d) -> p h d", h=4))
                P.I('dve', 'tensor_tensor', out=xdt[:].map(lambda a: a.rearrange("p (h d) -> p h d", h=4)), in0=x3,
                    in1=dt[:].map(lambda a: a.unsqueeze(2).broadcast_to([128, 4, 64])), op=ALU.mult)
                P.I('dve', 'tensor_tensor', out=xdtd[:].map(lambda a: a.rearrange("p (h d) -> p h d", h=4)), in0=x3,
                    in1=w2[:].map(lambda a: a.unsqueeze(2).broadcast_to([128, 4, 64])), op=ALU.mult)
                for hh in range(4):
                    P.I('pe', 'matmul', out=Yp[:, hh * 64:(hh + 1) * 64], lhsT=MT[:, hh, :], rhs=xdt[:, hh * 64:(hh + 1) * 64],
                        start=True, stop=True)
                P.I('pe', 'matmul', out=Yoff[:], lhsT=Cs[:, cs_], rhs=prevb[:], start=True, stop=True)
                P.I('pe', 'matmul', out=Stp[:], lhsT=Bpm[:], rhs=xdtd[:], start=True, stop=True)
                P.I('act', 'copy', out=ysb[:], in_=Yp[:])
                P.I('dve', 'tensor_tensor', out=ytmp[:].map(lambda a: a.rearrange("p (h d) -> p h d", h=4)),
                    in0=Yoff[:].map(lambda a: a.rearrange("p (h d) -> p h d", h=4)),
                    in1=ecs[:].map(lambda a: a.unsqueeze(2).broadcast_to([128, 4, 64])), op=ALU.mult)
                P.I('dve', 'tensor_tensor', out=ysb[:], in0=ysb[:], in1=ytmp[:], op=ALU.add)
                P.I('dve', 'tensor_tensor', out=ytmp[:], in0=xpm[:], in1=dsk, op=ALU.mult)
                P.I('dve', 'tensor_tensor', out=ysb[:], in0=ysb[:], in1=ytmp[:], op=ALU.add)
                P.I('dve', 'tensor_tensor', out=ysb[:], in0=ysb[:], in1=szs[:], op=ALU.mult)
                P.I('act', 'activation', out=ysq[:], in_=ysb[:], func=AF.Square, accum_out=ssq1[:])
                P.I('dve', 'tensor_copy', out=ssqc[:, blk:blk + 1], in_=ssq1[:])
                P.I('pool', 'tensor_tensor', out=ynb[:], in0=ysb[:], in1=nw, op=ALU.mult)
                for cc in range(2):
                    P.I('pe', 'transpose', out=ptr[:, 1 + cc, :], in_=ynb[:, cc * 128:(cc + 1) * 128], identity=identb[:])
                P.I('act', 'copy', out=mx[:, 1:3, cs_], in_=ptr[:, 1:3, :])
                P.I('dve', 'tensor_tensor', out=hT[:].map(lambda a: a.rearrange("p (h d) -> p h d", h=4)),
                    in0=hT[:].map(lambda a: a.rearrange("p (h d) -> p h d", h=4)),
                    in1=cdec[:].map(lambda a: a.unsqueeze(2).broadcast_to([128, 4, 64])), op=ALU.mult)
                P.I('dve', 'tensor_tensor', out=hT[:], in0=hT[:], in1=Stp[:], op=ALU.add)
                P.I('pool', 'tensor_copy', out=prevb[:], in_=hT[:])
            P.dma('pool', out=mixB[:, :, b0 * 128:b0 * 128 + N].map(lambda a: a.rearrange("c p n -> p c n")),
                  in_=mx[:, :, 0:N])
            recC.append(P.rec)
            P.rec = None
        P.replay_merged(recT[0], [])
        for ti in range(len(tl)):
            P.replay_merged(recC[ti], recT[ti + 1] if ti + 1 < len(tl) else [])
        P.I('pe', 'transpose', out=pz[:, 0:128], in_=ssqc[:], identity=ident)
        P.I('dve', 'tensor_copy', out=ssqT[:], in_=pz[:, 0:128])
        P.dma('pool', out=ssqd.full(), in_=ssqT[0:NB, :])
        P.barrier()


NCA = 3088
NCKV = 640


def slot_blocks_max(NB):
    NJ = (NB - 1) // 8
    return [0] + [8 * j + 8 for j in range(NJ)]


def rope(P, X, H, Dh, hf, cos, sin, tmp):
    x3 = X.map(lambda a: a.rearrange("p (h d) -> p h d", h=H))
    a_ = x3[:, :, 0:hf]
    b_ = x3[:, :, hf:2 * hf]
    r3 = lambda v: v.map(lambda a: a.rearrange("p (h d) -> p h d", h=H))
    c3, s3 = r3(cos), r3(sin)
    t1, t2, t3, t4 = [r3(t) for t in tmp]
    P.I('dve', 'tensor_tensor', out=t1, in0=a_, in1=c3, op=ALU.mult)
    P.I('dve', 'tensor_tensor', out=t2, in0=b_, in1=s3, op=ALU.mult)
    P.I('pool', 'tensor_tensor', out=t3, in0=a_, in1=s3, op=ALU.mult)
    P.I('pool', 'tensor_tensor', out=t4, in0=b_, in1=c3, op=ALU.mult)
    P.I('dve', 'tensor_tensor', out=a_, in0=t1, in1=t2, op=ALU.subtract)
    P.I('dve', 'tensor_tensor', out=b_, in0=t3, in1=t4, op=ALU.add)


def build_p3(NB):
    P = Prog()
    NS = 1 + (NB - 1) // 8
    PP = NB * 128
    io = dict(
        uT=P.din("uT", [NB, NK, 128, 128], BF16),
        uTo=P.din("uTo", [NS, NK, 128, 128], BF16),
        wA=P.din("wA", [D, NCA], F32),
        wKV=P.din("wKV", [D, NCKV], F32),
        prew=P.din("prew", [128, NK], F32),
        ropek=P.din("ropek", [NB, 128, 96], F32),
        ropeq=P.din("ropeq", [NS, 128, 512], F32),
        qpos=P.din("qpos", [128, NS], F32),
        kpos=P.din("kpos", [128, PP], F32),
        consts=P.din("consts", [128, 128], F32),
        attnT=P.dout("attnT", [NS, 8, 128, 128], BF16),
    )
    emit_p3(P, NB, io)
    return P.finish()


def emit_p3(P, NB, io, NS, sbm, tag=""):
    PP = NB * 128
    qTd = P.dint("p3_qTd" + tag, [NS, 128, 8, 128], BF16)
    qiTd = P.dint("p3_qiTd" + tag, [NS, 128, 8, 128], BF16)
    sgd = P.dint("p3_sgd" + tag, [NS, 128, 1024], BF16)
    wid = P.dint("p3_wid" + tag, [NS, 128, 16], F32)
    with contextlib.ExitStack() as st:
        sb = lambda n, s, d, dma=False, stack=st: P.sb("p3_" + n, s, d, dma=dma, stack=stack)
        prew = sb("prew", [128, NK], F32, True)
        consts = sb("consts", [128, 128], F32, True)
        identb = sb("identb", [128, 128], BF16)
        I4 = sb("I4", [128, 4, 128], BF16)
        P.dma('sp', out=prew[:], in_=io['prew'].full())
        P.dma('sp', out=consts[:], in_=io['consts'].full())
        P.I('dve', 'tensor_copy', out=identb[:], in_=consts[:])
        for i in range(4):
            P.I('dve', 'tensor_copy', out=I4[:, i, :], in_=consts[:])
        P.mark("p3a_A")
        with contextlib.ExitStack() as sa:
            sba = lambda n, s, d, dma=False: sb(n, s, d, dma, sa)
            wabf = sba("wabf", [128, NK, NCA], BF16)
            stg = [sba(f"stga{i}", [128, 1024], F32, True) for i in range(2)]
            io['wldA'](P, wabf, prew, stg)
            uto = [sba(f"uto{i}", [128, NK, 128], BF16, True) for i in range(2)]
            rq = [sba(f"rq{i}", [128, 512], F32, True) for i in range(2)]
            qpm = sba("qpm", [128, 1024], F32)
            ipm = sba("ipm", [128, 1024], F32)
            qb = sba("qb", [128, 1024], BF16)
            ib = sba("ib", [128, 1024], BF16)
            sgt = sba("sgt", [128, 1024], BF16)
            wit = sba("wit", [128, 16], F32)
            tmps = [sba(f"rt{i}", [128, 128], F32) for i in range(4)]
            qTs = sba("qTs", [128, 8, 128], BF16)
            qiTs = sba("qiTs", [128, 8, 128], BF16)
            pa = [P.ps(f"p3_pa{i}", [128, 512], F32, stack=sa) for i in range(4)]
            ptq = [P.ps(f"p3_ptq{i}", [128, 8, 128], BF16, stack=sa) for i in range(2)]
            pai = 0
            for s in range(NS):
                u = uto[s % 2]
                r = rq[s % 2]
                P.dma('sp', out=u[:], in_=io['uTo'][s].map(lambda a: a.rearrange("k d n -> d k n")))
                P.dma('sp', out=r[:], in_=io['ropeq'][s])

                def aproj(col0, n):
                    nonlocal pai
                    p = pa[pai % 4]
                    pai += 1
                    for k in range(NK):
                        P.I('pe', 'matmul', out=p[:, 0:n], lhsT=u[:, k, :], rhs=wabf[:, k, col0:col0 + n],
                            start=(k == 0), stop=(k == NK - 1))
                    return p
                for t in range(2):
                    p = aproj(t * 512, 512)
                    P.I('act', 'copy', out=qpm[:, t * 512:(t + 1) * 512], in_=p[:])
                for t in range(2):
                    p = aproj(1024 + t * 512, 512)
                    P.I('act', 'activation', out=sgt[:, t * 512:(t + 1) * 512], in_=p[:], func=AF.Silu)
                for t in range(2):
                    p = aproj(2048 + t * 512, 512)
                    P.I('act', 'copy', out=ipm[:, t * 512:(t + 1) * 512], in_=p[:])
                p = aproj(3072, 16)
                P.I('dve', 'tensor_copy', out=wit[:], in_=p[:, 0:16])
                rope(P, qpm[:], 8, 128, 16, r[:, 0:128], r[:, 128:256], [t[:] for t in tmps])
                rope(P, ipm[:], 16, 64, 8, r[:, 256:384], r[:, 384:512], [t[:] for t in tmps])
                P.I('dve', 'tensor_copy', out=qb[:], in_=qpm[:])
                P.I('pool', 'tensor_copy', out=ib[:], in_=ipm[:])
                for hh in range(8):
                    P.I('pe', 'transpose', out=ptq[0][:, hh, :], in_=qb[:, hh * 128:(hh + 1) * 128], identity=identb[:])
                P.I('dve', 'tensor_copy', out=qTs[:], in_=ptq[0][:])
                for hh in range(8):
                    P.I('pe', 'transpose', out=ptq[1][:, hh, :], in_=ib[:, hh * 128:(hh + 1) * 128], identity=identb[:])
                P.I('act', 'copy', out=qiTs[:], in_=ptq[1][:])
                P.dma('pool', out=qTd[s], in_=qTs[:])
                P.dma('pool', out=qiTd[s], in_=qiTs[:])
                P.dma('pool', out=sgd[s], in_=sgt[:])
                P.dma('pool', out=wid[s], in_=wit[:])
            P.barrier()
        P.mark("p3a_KV")
        kT = sb("kT", [128, 2, PP], BF16)
        Vx = sb("Vx", [128, NB, 2, 129], BF16)
        kiT = sb("kiT", [128, PP], BF16)
        P.I('pool', 'memset', ap=Vx[:], constant=1.0)
        with contextlib.ExitStack() as sk:
            sbk = lambda n, s, d, dma=False: sb(n, s, d, dma, sk)
            wkbf = sbk("wkbf", [128, NK, NCKV], BF16)
            stg = [sbk(f"stgk{i}", [128, NCKV], F32, True) for i in range(2)]
            io['wldKV'](P, wkbf, prew, stg)
            ut = [sbk(f"ut{i}", [128, NK, 512], BF16, True) for i in range(2)]
            rk = [sbk(f"rk{i}", [128, 4, 96], F32, True) for i in range(2)]
            kpm = sbk("kpm", [128, 256], F32)
            kipm = sbk("kipm", [128, 128], F32)
            kb = sbk("kb", [128, 384], BF16)
            tmps = [sbk(f"rtk{i}", [128, 32], F32) for i in range(4)]
            pk = [P.ps(f"p3_pk{i}", [128, 512], F32, stack=sk) for i in range(2)]
            pk2 = [P.ps(f"p3_pk2{i}", [128, 128], F32, stack=sk) for i in range(2)]
            ptk = P.ps("p3_ptk", [128, 3, 128], BF16, stack=sk)
            tl = tiles_of(NB)

            def ldk(ti):
                b0, nb = tl[ti]
                load_ut_tile(P, ut[ti % 2], io['uT'], b0, nb)
                P.dma('sp', out=rk[ti % 2][:, 0:nb, :], in_=io['ropek'][b0:b0 + nb].map(lambda a: a.rearrange("b p c -> p b c")))
            ldk(0)
            it = 0
            for ti, (b0, nb) in enumerate(tl):
                if ti + 1 < len(tl):
                    ldk(ti + 1)
                u = ut[ti % 2]
                r = rk[ti % 2]
                for bb in range(nb):
                    blk = b0 + bb
                    cs_ = slice(bb * 128, bb * 128 + 128)
                    p1, p2 = pk[it % 2], pk2[it % 2]
                    it += 1
                    for k in range(NK):
                        P.I('pe', 'matmul', out=p1[:], lhsT=u[:, k, cs_], rhs=wkbf[:, k, 0:512], start=(k == 0), stop=(k == NK - 1))
                    for k in range(NK):
                        P.I('pe', 'matmul', out=p2[:], lhsT=u[:, k, cs_], rhs=wkbf[:, k, 512:640], start=(k == 0), stop=(k == NK - 1))
                    P.I('act', 'copy', out=kpm[:], in_=p1[:, 0:256])
                    P.I('act', 'copy', out=Vx[:, blk, :, 0:128], in_=p1[:, 256:512].map(lambda a: a.rearrange("p (g d) -> p g d", g=2)))
                    P.I('act', 'copy', out=kipm[:], in_=p2[:])
                    rope(P, kpm[:], 2, 128, 16, r[:, bb, 0:32], r[:, bb, 32:64], [t[:] for t in tmps])
                    rope(P, kipm[:], 2, 64, 8, r[:, bb, 64:80], r[:, bb, 80:96], [t[:, 0:16] for t in tmps])
                    P.I('dve', 'tensor_copy', out=kb[:, 0:256], in_=kpm[:])
                    P.I('pool', 'tensor_copy', out=kb[:, 256:384], in_=kipm[:])
                    for i in range(3):
                        P.I('pe', 'transpose', out=ptk[:, i, :], in_=kb[:, i * 128:(i + 1) * 128], identity=identb[:])
                    P.I('dve', 'tensor_copy', out=kT[:, :, blk * 128:(blk + 1) * 128], in_=ptk[:, 0:2, :])
                    P.I('act', 'copy', out=kiT[:, blk * 128:(blk + 1) * 128], in_=ptk[:, 2, :])
            P.barrier()
        P.mark("p3b")
        with contextlib.ExitStack() as sq_:
            sbq = lambda n, s, d, dma=False: sb(n, s, d, dma, sq_)
            qpos = sbq("qpos", [128, NS], F32, True)
            kcol = sbq("kcol", [128, 1024], F32, True)
            P.dma('sp', out=qpos[:], in_=io['qpos'].full())
            P.dma('sp', out=kcol[:], in_=io['kpos'][:, 0:1024])
            qT = [sbq(f"qT{i}", [128, 8, 128], BF16, True) for i in range(2)]
            qiT = [sbq(f"qiT{i}", [128, 8, 128], BF16, True) for i in range(2)]
            sgq = [sbq(f"sgq{i}", [128, 1024], BF16, True) for i in range(2)]
            wiq = [sbq(f"wiq{i}", [128, 16], F32, True) for i in range(2)]
            Wd = sbq("Wd", [128, 16, 128], BF16)
            pw2 = sbq("pw2", [128, NBIS + 1], F32)
            tabA = sbq("tabA", [128, NBIS + 1], F32)
            tabB = sbq("tabB", [128, NBIS + 1], F32)
            for k_ in range(NBIS + 1):
                P.I('dve', 'memset', ap=pw2[:, k_:k_ + 1], constant=float(2.0 ** -k_))
            score = sbq("score", [128, PP], F32)
            MBs = [sbq(f"MB{i}", [128, PP], BF16) for i in range(2)]
            pen = sbq("pen", [128, 1024], F32)
            R = [sbq(f"R{i}", [128, 512], BF16) for i in range(8)]
            PT = [sbq(f"PT{i}", [128, 512], BF16) for i in range(4)]
            att = sbq("att", [128, 1024], BF16)
            attT = [sbq(f"attT{i}", [128, 8, 128], BF16) for i in range(2)]
            col = {n: sbq("c_" + n, [128, 1], F32) for n in ("m", "lo", "wh", "mid", "cnt", "g", "rinv", "qrel")}
            OA = P.ps("p3_OA", [128, 512], F32, stack=sq_)
            OB = P.ps("p3_OB", [128, 512], F32, stack=sq_)
            Osl = [OA[:, 0:129], OA[:, 129:258], OA[:, 258:387], OB[:, 0:129]]
            Lp = P.ps("p3_L", [128, 512], F32, stack=sq_)
            Sb = [P.ps(f"p3_S{i}", [128, 512], F32, stack=sq_) for i in range(4)]
            Sall = Sb + [OA, OB, Lp]
            Lall = [Lp] + Sb
            SC = P.ps("p3_SC", [128, 512], F32, stack=sq_)
            scale = 128 ** -0.5
            cnt_ = dict(si=0, ri=0, li=0)

            def ldq(s):
                i = s % 2
                P.dma('sp', out=qT[i][:], in_=qTd[s])
                P.dma('sp', out=qiT[i][:], in_=qiTd[s])
                P.dma('sp', out=sgq[i][:], in_=sgd[s])
                P.dma('sp', out=wiq[i][:], in_=wid[s])

            def nkb_of(s):
                return min(sbm[s] + 1, NB)

            def indexer(s):
                i = s % 2
                nkb = nkb_of(s)
                P.I('dve', 'tensor_tensor', out=Wd[:],
                    in0=identb[:].map(lambda a: a.unsqueeze(1).broadcast_to([128, 16, 128])),
                    in1=wiq[i][:].map(lambda a: a.unsqueeze(2).broadcast_to([128, 16, 128])), op=ALU.mult)
                for (t0, tn) in tiles_of(nkb):
                    c0, n = t0 * 128, tn * 128
                    pend = None
                    for pr_ in range(9):
                        cur = None
                        if pr_ < 8:
                            cur = []
                            for j in range(2):
                                hd = 2 * pr_ + j
                                p = Sall[cnt_['si'] % 7]
                                cnt_['si'] += 1
                                lo_ = j * 64
                                P.I('pe', 'matmul', out=p[:, 0:n], lhsT=qiT[i][lo_:lo_ + 64, pr_, :], rhs=kiT[lo_:lo_ + 64, c0:c0 + n],
                                    start=True, stop=True)
                                cur.append((hd, p))
                        if pend is not None:
                            for (ph, r) in pend:
                                P.I('pe', 'matmul', out=SC[:, 0:n], lhsT=Wd[:, ph, :], rhs=r[:, 0:n], start=(ph == 0), stop=(ph == 15))
                        pend = None
                        if cur is not None:
                            pend = []
                            for j, (hd, p) in enumerate(cur):
                                r = R[cnt_['ri'] % 8]
                                cnt_['ri'] += 1
                                if j == 0:
                                    P.I('act', 'activation', out=r[:, 0:n], in_=p[:, 0:n], func=AF.Relu)
                                else:
                                    P.I('dve', 'tensor_scalar', out=r[:, 0:n], in0=p[:, 0:n], scalar1=0.0, scalar2=None, op0=ALU.max)
                                pend.append((hd, r))
                    P.I('act', 'copy', out=score[:, c0:c0 + n], in_=SC[:, 0:n])

            def bisect(s):
                nkb = nkb_of(s)
                KE = nkb * 128
                MB = MBs[s % 2]
                P.I('dve', 'tensor_scalar', out=MB[:, 0:KE], in0=score[:, 0:KE], scalar1=1.0, scalar2=-1e30,
                    op0=ALU.mult, op1=ALU.max, accum_out=col['m'][:])
                P.I('dve', 'tensor_scalar', out=MB[:, 0:KE], in0=score[:, 0:KE], scalar1=-1.0, scalar2=-1e30,
                    op0=ALU.mult, op1=ALU.max, accum_out=col['mid'][:])
                P.I('dve', 'tensor_tensor', out=col['m'][:], in0=col['m'][:], in1=col['mid'][:], op=ALU.max)
                r0 = max(0, KE - 1024)
                rn = KE - r0
                P.I('dve', 'tensor_scalar', out=col['qrel'][:], in0=qpos[:, s:s + 1], scalar1=float(-r0), scalar2=None, op0=ALU.add)
                P.I('dve', 'tensor_scalar', out=pen[:, 0:rn], in0=kcol[:, 0:rn], scalar1=col['qrel'][:], scalar2=-1e30,
                    op0=ALU.is_gt, op1=ALU.mult)
                P.I('dve', 'tensor_tensor', out=score[:, r0:KE], in0=score[:, r0:KE], in1=pen[:, 0:rn], op=ALU.add)
                P.I('dve', 'memset', ap=score[:, 0:PADN], constant=-1e30)
                P.I('dve', 'tensor_scalar', out=col['wh'][:], in0=col['m'][:], scalar1=1.0, scalar2=2.0, op0=ALU.add, op1=ALU.mult)
                P.I('dve', 'tensor_scalar', out=tabA[:], in0=pw2[:], scalar1=col['wh'][:], scalar2=-0.25, op0=ALU.mult, op1=ALU.mult)
                P.I('dve', 'tensor_scalar', out=tabB[:], in0=pw2[:], scalar1=col['wh'][:], scalar2=0.5, op0=ALU.mult, op1=ALU.mult)
                P.I('dve', 'memset', ap=col['mid'][:], constant=0.0)
                for itn in range(NBIS):
                    P.I('dve', 'tensor_scalar', out=MB[:, 0:KE], in0=score[:, 0:KE], scalar1=col['mid'][:], scalar2=0.0,
                        op0=ALU.is_ge, op1=ALU.add, accum_out=col['cnt'][:])
                    P.I('dve', 'tensor_scalar', out=col['g'][:], in0=col['cnt'][:], scalar1=TOPK - 0.5, scalar2=tabB[:, itn:itn + 1],
                        op0=ALU.is_ge, op1=ALU.mult)
                    P.I('dve', 'scalar_tensor_tensor', out=col['mid'][:], in0=col['mid'][:], scalar=tabA[:, itn:itn + 1], in1=col['g'][:],
                        op0=ALU.add, op1=ALU.add)
                P.I('dve', 'scalar_tensor_tensor', out=col['lo'][:], in0=tabA[:, NBIS:NBIS + 1], scalar=2.0, in1=col['mid'][:],
                    op0=ALU.mult, op1=ALU.add)
                P.I('dve', 'tensor_scalar', out=MB[:, 0:KE], in0=score[:, 0:KE], scalar1=col['lo'][:], scalar2=NEG,
                    op0=ALU.is_lt, op1=ALU.mult)

            def attention(s):
                i = s % 2
                nkb = nkb_of(s)
                MB = MBs[s % 2]
                for g in range(2):
                    for sbk_ in range(nkb):
                        ks = slice(sbk_ * 128, sbk_ * 128 + 128)
                        pt = PT[cnt_['li'] % 4]
                        Lb = Lall[cnt_['li'] % 5]
                        cnt_['li'] += 1
                        P.I('pe', 'matmul', out=Lb[:], lhsT=kT[:, g, ks], rhs=qT[i][:, 4 * g:4 * g + 4, :], start=True, stop=False)
                        P.I('pe', 'matmul', out=Lb[:], lhsT=MB[:, ks], rhs=I4[:], start=False, stop=True)
                        P.I('act', 'activation', out=pt[:], in_=Lb[:], func=AF.Exp, scale=scale)
                        for hh in range(4):
                            P.I('pe', 'matmul', out=Osl[hh], lhsT=pt[:, hh * 128:(hh + 1) * 128], rhs=Vx[:, sbk_, g, :],
                                start=(sbk_ == 0 and hh in (0, 3)), stop=(sbk_ == nkb - 1), skip_group_check=(hh < 3))
                    for hh in range(4):
                        h_ = 4 * g + hh
                        P.I('dve', 'tensor_scalar', out=col['rinv'][:], in0=Osl[hh][:, 128:129], scalar1=1e-30, scalar2=None, op0=ALU.add)
                        P.I('dve', 'reciprocal', out=col['rinv'][:], in_=col['rinv'][:])
                        P.I('dve', 'scalar_tensor_tensor', out=att[:, h_ * 128:(h_ + 1) * 128], in0=Osl[hh][:, 0:128],
                            scalar=col['rinv'][:], in1=sgq[i][:, h_ * 128:(h_ + 1) * 128], op0=ALU.mult, op1=ALU.mult)
                ptt = OA.full().map(lambda a: a.bitcast(BF16).rearrange("p (h n) -> p h n", h=8))
                for hh in range(8):
                    P.I('pe', 'transpose', out=ptt[:, hh, :], in_=att[:, hh * 128:(hh + 1) * 128], identity=identb[:])
                aT = attT[s % 2]
                P.I('act', 'copy', out=aT[:], in_=ptt)
                P.dma('pool', out=io['attnT'][s].map(lambda a: a.rearrange("c p n -> p c n")), in_=aT[:])

            ldq(0)
            for s in range(NS + 1):
                if s < NS:
                    indexer(s)
                    bisect(s)
                if s >= 1:
                    attention(s - 1)
                if s + 1 < NS:
                    ldq(s + 1)
            P.barrier()


def build_p4(NB):
    P = Prog()
    NS = 1 + (NB - 1) // 8
    PP = NB * 128
    io = dict(
        mixo=P.din("mixo", [NS, 24, 128, 128], BF16),
        ssqo=P.din("ssqo", [NS, 8, 128], F32),
        attnT=P.din("attnT", [NS, 8, 128, 128], BF16),
        h=P.din("h", [NS, 128, D], F32),
        wout=P.din("wout", [32, 128, D], F32),
        postw=P.din("postw", [1, D], F32),
        rowvalid=P.din("rowvalid", [128, NS], F32),
        consts=P.din("consts", [128, 128], F32),
        hout=P.dout("hout", [NS, 128, D], F32),
    )
    emit_p4(P, NB, io)
    return P.finish()


def emit_p4(P, NB, io, NS):
    P.mark("p4")
    with contextlib.ExitStack() as st:
        sb = lambda n, s, d, dma=False: P.sb("p4_" + n, s, d, dma=dma, stack=st)
        wbf = sb("wbf", [128, 32, D], BF16)
        stg = [sb(f"stg{i}", [128, 1024], F32, True) for i in range(2)]
        postw = sb("postw", [128, D], F32, True)
        rowv = sb("rowv", [128, NS], F32, True)
        ident = sb("ident", [128, 128], F32, True)
        P.dma('sp', out=postw[:], in_=io['postw'].full().map(lambda a: a.broadcast_to([128, D])))
        P.dma('sp', out=rowv[:], in_=io['rowvalid'].full())
        P.dma('sp', out=ident[:], in_=io['consts'].full())
        for c2 in range(64):
            c, hf = c2 // 2, c2 % 2
            s = stg[c2 % 2]
            P.dma('sp', out=s[:], in_=io['wout'][c, :, hf * 1024:(hf + 1) * 1024])
            eng = ('dve', 'pool', 'act')[c2 % 3]
            if eng == 'act':
                P.I('act', 'copy', out=wbf[:, c, hf * 1024:(hf + 1) * 1024], in_=s[:])
            else:
                P.I(eng, 'tensor_copy', out=wbf[:, c, hf * 1024:(hf + 1) * 1024], in_=s[:])
        mt = [sb(f"mt{i}", [128, 32, 128], BF16, True) for i in range(2)]
        ht = [sb(f"ht{i}", [128, D], F32, True) for i in range(2)]
        sq8 = [sb(f"sq8{i}", [8, 128], F32, True) for i in range(2)]
        acc = sb("acc", [128, D], F32)
        junk = sb("junk", [128, D], BF16)
        hn = [sb("hn0", [128, D], F32)] * 2
        rsg = sb("rsg", [128, 4], F32)
        sqT = sb("sqT", [128, 8], F32)
        ss = sb("ss", [128, 1], F32)
        rs = sb("rs", [128, 1], F32)
        Po = P.ps("p4_Po", [128, 512], F32, stack=st)
        Pg = [P.ps(f"p4_Pg{i}", [128, 512], F32, stack=st) for i in range(4)]
        pq = P.ps("p4_pq", [128, 8], F32, stack=st)

        def ld(s):
            i = s % 2
            P.dma('sp', out=mt[i][:, 0:24, :], in_=io['mixo'](s))
            P.dma('sp', out=mt[i][:, 24:32, :], in_=io['attnT'][s].map(lambda a: a.rearrange("c p n -> p c n")))
            P.dma('sp', out=ht[i][:], in_=io['h'][s])
            P.dma('sp', out=sq8[i][:], in_=io['ssqo'](s))
        ld(0)
        for s in range(NS):
            if s + 1 < NS:
                ld(s + 1)
            i = s % 2
            m = mt[i]
            P.I('pe', 'transpose', out=pq[:], in_=sq8[i][:], identity=ident[0:8, 0:8])
            P.I('dve', 'tensor_copy', out=sqT[:], in_=pq[:])
            s3 = sqT[:].map(lambda a: a.rearrange("p (g t) -> p g t", t=2))
            P.I('dve', 'tensor_tensor', out=rsg[:], in0=s3[:, :, 0], in1=s3[:, :, 1], op=ALU.add)
            rsqrt_small(P, rsg[:], rsg[:], 1.0 / 512, EPS)
            for dtile in range(4):
                ds = slice(dtile * 512, dtile * 512 + 512)
                lst = [(3 * c, c) for c in range(8)] + [(24 + j, 24 + j) for j in range(8)]
                for n_, (mi, wi_) in enumerate(lst):
                    P.I('pe', 'matmul', out=Po[:], lhsT=m[:, mi, :], rhs=wbf[:, wi_, ds], start=(n_ == 0), stop=(n_ == len(lst) - 1))
                for g in range(4):
                    lst = [(3 * c + ci, 8 + 2 * c + (ci - 1)) for c in (2 * g, 2 * g + 1) for ci in (1, 2)]
                    for n_, (mi, wi_) in enumerate(lst):
                        P.I('pe', 'matmul', out=Pg[g][:], lhsT=m[:, mi, :], rhs=wbf[:, wi_, ds], start=(n_ == 0), stop=(n_ == len(lst) - 1))
                P.I('act', 'copy', out=acc[:, ds], in_=Po[:])
                for g in range(4):
                    P.I('dve', 'scalar_tensor_tensor', out=acc[:, ds], in0=Pg[g][:], scalar=rsg[:, g:g + 1], in1=acc[:, ds],
                        op0=ALU.mult, op1=ALU.add)
            P.I('act', 'activation', out=junk[:], in_=acc[:], func=AF.Square, accum_out=ss[:])
            rsqrt_small(P, rs[:], ss[:], 1.0 / D, EPS)
            o = hn[i]
            P.I('dve', 'scalar_tensor_tensor', out=o[:], in0=acc[:], scalar=rs[:], in1=postw[:], op0=ALU.mult, op1=ALU.mult)
            P.I('pool', 'tensor_tensor', out=o[:], in0=o[:], in1=ht[i][:], op=ALU.add)
            if s == 0:
                P.I('dve', 'tensor_scalar', out=o[:], in0=o[:], scalar1=rowv[:, 0:1], scalar2=None, op0=ALU.mult)
            P.dma('pool', out=io['hout'][s], in_=o[:])
        P.barrier()


def reg(buf, *idx):
    return Region(buf, buf._base()[idx] if idx else buf._base())


def build_fused(NB, depth, phases="1234"):
    P = Prog()
    PP = NB * 128
    h0 = P.din("h0", [NB, 128, D], F32)
    w_in = P.din("w_in", [depth, D, 10864], F32)
    w_out = P.din("w_out", [depth, 32, 128, D], F32)
    prew = P.din("prew", [depth, 128, NK], F32)
    postw = P.din("postw", [depth, 1, D], F32)
    poolw = P.din("poolw", [depth, 8, 256, 128], F32)
    pvec = P.din("pvec", [depth, 8, 128, 32], F32)
    convw = P.din("convw", [depth, 8, 128, 16], F32)
    rows = P.din("rows", [depth, 8, 128, 520], F32)
    consts2 = P.din("consts2", [128, 384], F32)
    ident = P.din("ident", [128, 128], F32)
    ropek = P.din("ropek", [NB, 128, 96], F32)
    ropeq = P.din("ropeq", [NB, 128, 512], F32)
    qpos = P.din("qpos", [128, NB], F32)
    kpos = P.din("kpos", [128, PP], F32)
    rowvalid = P.din("rowvalid", [128, NB], F32)
    hout = P.dout("hout", [NB, 128, D], F32)
    uT = P.dint("f_uT", [NB, NK, 128, 128], BF16)
    mixB = P.dint("f_mixB", [8, 3, 128, PP], BF16)
    ssq = P.dint("f_ssq", [8, NB, 128], F32)
    attnT = P.dint("f_attnT", [NB, 8, 128, 128], BF16)
    hmid = [P.dint("f_h%d" % i, [NB, 128, D], F32) for i in range(max(depth - 1, 0))]
    sbm = list(range(NB))
    for l in range(depth):
        hin = h0 if l == 0 else hmid[l - 1]
        ho = hout if l == depth - 1 else hmid[l]
        wl = reg(w_in, l)
        if '1' in phases:
            emit_p1(P, NB, hin, ident, uT)
        for c in range(NCORES):
            g = c // 2
            ranges = [(O_PV + g * 256, 256, 0), (O_PG + c * 128, 128, 256), (O_X + 256 * c, 256, 384),
                      (O_B + 128 * g, 128, 640), (O_C + 128 * g, 128, 768), (O_Z + 256 * c, 256, 896),
                      (O_DT + 4 * c, 4, 1152)]
            io = dict(uT=uT, wld=ranged_loader(wl, ranges), prew=reg(prew, l), poolw=reg(poolw, l, c), pvec=reg(pvec, l, c),
                      convw=reg(convw, l, c), rows=reg(rows, l, c), consts=consts2, mixB=reg(mixB, c), ssq=reg(ssq, c))
            if '2' in phases:
                emit_p2(P, NB, io)
        io = dict(uT=uT, uTo=uT, prew=reg(prew, l), ropek=ropek, ropeq=ropeq, qpos=qpos, kpos=kpos, consts=ident, attnT=attnT,
                  wldA=ranged_loader(wl, [(O_Q, 1024, 0), (O_G, 1024, 1024), (O_IQ, 1024, 2048), (O_IW, 16, 3072)]),
                  wldKV=ranged_loader(wl, [(O_K, 256, 0), (O_V, 256, 256), (O_IK, 64, 512), (O_IK, 64, 576)]))
        if '3' in phases:
            emit_p3(P, NB, io, NB, sbm, tag="_%d" % l)
        io = dict(mixo=lambda s_: mixB[:, :, :, s_ * 128:(s_ + 1) * 128].map(lambda a: a.rearrange("v c p n -> p (v c) n")),
                  ssqo=lambda s_: ssq[:, s_, :], attnT=attnT, h=hin, wout=reg(w_out, l), postw=reg(postw, l),
                  rowvalid=rowvalid, consts=ident, hout=ho)
        if '4' in phases:
            emit_p4(P, NB, io, NB)
    P.mark("end")
    global LAST_MARKS
    LAST_MARKS = P.marks
    print("fused program: ecnt", P.ecnt, "nwait", P.nwait, flush=True)
    return P.finish()


_PROGS = {}


def rope_tables(NB):
    PP = NB * 128
    pos = np.maximum(np.arange(PP) - PADN, 0).astype(np.float32)

    def cs(rot):
        half = rot // 2
        inv = np.power(np.float32(500000.0), -(np.arange(half, dtype=np.float32) * 2.0 / rot)).astype(np.float32)
        ang = pos[:, None] * inv[None, :]
        return np.cos(ang).astype(np.float32), np.sin(ang).astype(np.float32)
    c32, s32 = cs(32)
    c16, s16 = cs(16)
    ropek = np.concatenate([np.tile(c32, (1, 2)), np.tile(s32, (1, 2)), np.tile(c16, (1, 2)), np.tile(s16, (1, 2))], axis=1)
    ropeq = np.concatenate([np.tile(c32, (1, 8)), np.tile(s32, (1, 8)), np.tile(c16, (1, 16)), np.tile(s16, (1, 16))], axis=1)
    return ropek.reshape(NB, 128, 96), ropeq.reshape(NB, 128, 512)


def consts_p2():
    ident = np.eye(128, dtype=np.float32)
    s = np.arange(128)
    U = (s[:, None] <= s[None, :]).astype(np.float32)
    MBT = np.where(s[None, :] >= s[:, None], 0.0, NEG).astype(np.float32)
    return np.concatenate([ident, U, MBT], axis=1)


def run(nc, maps):
    res = run_bass_kernel_spmd(nc, maps, core_ids=list(range(len(maps))))
    return res.results


def small_params(inp, depth):
    windows = (2, 4, 8, 16)
    poolw = np.zeros((depth, 8, 256, 128), np.float32)
    pvec = np.zeros((depth, 8, 128, 32), np.float32)
    convw = np.zeros((depth, 8, 128, 16), np.float32)
    rows = np.zeros((depth, 8, 128, 520), np.float32)
    for l in range(depth):
        for c in range(8):
            g, half = c // 2, c % 2
            poolw[l, c] = inp['pool_w'][l][g][:, half * 128:(half + 1) * 128]
            pvec[l, c, :, 0] = inp['pool_scale'][l][c * 128:(c + 1) * 128]
            w = windows[g]
            pvec[l, c, :, 1 + g] = np.float32(1.0) / np.float32(w)
            j = np.arange(16)
            pvec[l, c, :, 5:21] = (np.float32(w) / np.minimum(w, j + 1).astype(np.float32))[None, :]
            ccols = [np.arange(256 * c, 256 * c + 128), np.arange(256 * c + 128, 256 * c + 256),
                     np.arange(2048 + 128 * g, 2048 + 128 * (g + 1)), np.arange(2560 + 128 * g, 2560 + 128 * (g + 1))]
            for ci, cc in enumerate(ccols):
                pvec[l, c, :, 21 + ci] = inp['conv_b'][l][cc]
                convw[l, c, :, ci * 4:(ci + 1) * 4] = inp['conv_w'][l][:, cc].T
            rows[l, c, :, 0:4] = inp['dt_bias'][l][4 * c:4 * c + 4][None]
            rows[l, c, :, 4:8] = inp['a_log'][l][4 * c:4 * c + 4][None]
            rows[l, c, :, 8:264] = np.repeat(inp['d_skip'][l][4 * c:4 * c + 4], 64)[None]
            rows[l, c, :, 264:520] = inp['ssd_norm_w'][l][256 * c:256 * (c + 1)][None]
    return poolw, pvec, convw, rows


def forward(inp, NB, depth, debug=None, ncores=NCORES):
    PP = NB * 128
    x = np.asarray(inp['x'], np.float32)[0]
    h0 = np.zeros((PP, D), np.float32)
    h0[PADN:PADN + NMETA] = inp['meta_tokens']
    h0[128:] = x
    ropek, ropeq = rope_tables(NB)
    poolw, pvec, convw, rows = small_params(inp, depth)
    rowvalid = np.ones((128, NB), np.float32)
    rowvalid[:PADN, 0] = 0.0
    m = dict(
        h0=h0.reshape(NB, 128, D),
        w_in=np.ascontiguousarray(inp['w_in'][:depth], np.float32),
        w_out=np.ascontiguousarray(inp['w_out'][:depth].reshape(depth, 32, 128, D), np.float32),
        prew=np.ascontiguousarray(inp['pre_norm_w'][:depth].reshape(depth, NK, 128).transpose(0, 2, 1)),
        postw=np.ascontiguousarray(inp['post_norm_w'][:depth].reshape(depth, 1, D)),
        poolw=poolw, pvec=pvec, convw=convw, rows=rows,
        consts2=consts_p2(), ident=np.eye(128, dtype=np.float32), ropek=ropek, ropeq=ropeq,
        qpos=np.ascontiguousarray((np.arange(128, dtype=np.float32)[:, None] + 128.0 * np.arange(NB, dtype=np.float32)[None, :])),
        kpos=np.ascontiguousarray(np.broadcast_to(np.arange(PP, dtype=np.float32)[None], (128, PP))),
        rowvalid=rowvalid)
    key = (NB, depth)
    if key not in _PROGS:
        _PROGS[key] = build_fused(NB, depth)
    res = run(_PROGS[key], [m] * ncores)
    hout = np.asarray(res[0]['hout'])
    return hout.reshape(PP, D)[128:][None]


def kernel(**inputs):
    inp = {k: np.asarray(v) for k, v in inputs.items()}
    S = inp['x'].shape[1]
    NB = (S + 128) // 128
    depth = inp['w_in'].shape[0]
    return forward(inp, NB, depth).astype(np.float32)


def slot_blocks_max8(NB):
    NJ = (NB - 1) // 8
    return [0] + [8 * j + 8 for j in range(NJ)]


def ub_p1(NS):
    P = Prog()
    h = P.din("h", [NS, 128, D], F32)
    identd = P.din("ident", [128, 128], F32)
    uT = P.dout("uT", [NS, NK, 128, 128], BF16)
    emit_p1(P, NS, h, identd, uT)
    return P.finish()


def ub_p2(NB):
    P = Prog()
    PP = NB * 128
    wB = P.din("wB", [D, NCB], F32)
    io = dict(uT=P.din("uT", [NB, NK, 128, 128], BF16), wld=ranged_loader(wB, [(0, NCB, 0)]),
              prew=P.din("prew", [128, NK], F32), poolw=P.din("poolw", [256, 128], F32), pvec=P.din("pvec", [128, 32], F32),
              convw=P.din("convw", [128, 16], F32), rows=P.din("rows", [128, 520], F32), consts=P.din("consts", [128, 384], F32),
              mixB=P.dout("mixB", [3, 128, PP], BF16), ssq=P.dout("ssq", [NB, 128], F32))
    emit_p2(P, NB, io)
    return P.finish()


def ub_p3(NB):
    P = Prog()
    NS = 1 + (NB - 1) // 8
    PP = NB * 128
    wA = P.din("wA", [D, NCA], F32)
    wKV = P.din("wKV", [D, NCKV], F32)
    io = dict(uT=P.din("uT", [NB, NK, 128, 128], BF16), uTo=P.din("uTo", [NS, NK, 128, 128], BF16),
              wldA=ranged_loader(wA, [(0, NCA, 0)]), wldKV=ranged_loader(wKV, [(0, NCKV, 0)]),
              prew=P.din("prew", [128, NK], F32), ropek=P.din("ropek", [NB, 128, 96], F32), ropeq=P.din("ropeq", [NS, 128, 512], F32),
              qpos=P.din("qpos", [128, NS], F32), kpos=P.din("kpos", [128, 1024], F32), consts=P.din("consts", [128, 128], F32),
              attnT=P.dout("attnT", [NS, 8, 128, 128], BF16))
    emit_p3(P, NB, io, NS, slot_blocks_max8(NB))
    return P.finish()


def ub_p4(NB):
    P = Prog()
    NS = 1 + (NB - 1) // 8
    mixo = P.din("mixo", [NS, 24, 128, 128], BF16)
    ssqo = P.din("ssqo", [NS, 8, 128], F32)
    io = dict(mixo=lambda s_: mixo[s_].map(lambda a: a.rearrange("c p n -> p c n")), ssqo=lambda s_: ssqo[s_],
              attnT=P.din("attnT", [NS, 8, 128, 128], BF16), h=P.din("h", [NS, 128, D], F32), wout=P.din("wout", [32, 128, D], F32),
              postw=P.din("postw", [1, D], F32), rowvalid=P.din("rowvalid", [128, NS], F32), consts=P.din("consts", [128, 128], F32),
              hout=P.dout("hout", [NS, 128, D], F32))
    emit_p4(P, NB, io, NS)
    return P.finish()


def own_blocks(c, NB):
    return [0] + [1 + c + 8 * j for j in range((NB - 1) // 8)]


def forward_unfused(inp, NB, depth):
    PP = NB * 128
    NS = 1 + (NB - 1) // 8
    x = np.asarray(inp['x'], np.float32)[0]
    h0 = np.zeros((PP, D), np.float32)
    h0[PADN:PADN + NMETA] = inp['meta_tokens']
    h0[128:] = x
    hb = h0.reshape(NB, 128, D)
    own = [own_blocks(c, NB) for c in range(NCORES)]
    hown = [np.ascontiguousarray(hb[own[c]]) for c in range(NCORES)]
    ident = np.eye(128, dtype=np.float32)
    ropek, ropeq = rope_tables(NB)
    kpos = np.ascontiguousarray(np.broadcast_to(np.arange(1024, dtype=np.float32)[None], (128, 1024)))
    c2 = consts_p2()
    poolw, pvec, convw, rows = small_params(inp, depth)
    qpos, rowvalid = [], []
    for c in range(NCORES):
        qpos.append(np.ascontiguousarray(np.stack([np.arange(128, dtype=np.float32) + 128 * b for b in own[c]], axis=1)))
        rv = np.ones((128, NS), np.float32)
        rv[:PADN, 0] = 0.0
        rowvalid.append(rv)
    progs = dict(p1=ub_p1(NS), p2=ub_p2(NB), p3=ub_p3(NB), p4=ub_p4(NB))
    colsA = np.concatenate([np.arange(O_Q, O_Q + 1024), np.arange(O_G, O_G + 1024), np.arange(O_IQ, O_IQ + 1024), np.arange(O_IW, O_IW + 16)])
    colsKV = np.concatenate([np.arange(O_K, O_K + 256), np.arange(O_V, O_V + 256), np.arange(O_IK, O_IK + 64), np.arange(O_IK, O_IK + 64)])
    for l in range(depth):
        w_in = inp['w_in'][l]
        prew = np.ascontiguousarray(inp['pre_norm_w'][l].reshape(NK, 128).T)
        r1 = run(progs['p1'], [dict(h=hown[c], ident=ident) for c in range(NCORES)])
        uTo = [np.asarray(r1[c]['uT']) for c in range(NCORES)]
        uT = np.zeros((NB, NK, 128, 128), uTo[0].dtype)
        for c in range(NCORES):
            uT[own[c]] = uTo[c]
        maps = []
        for c in range(NCORES):
            g = c // 2
            cols = np.concatenate([np.arange(O_PV + g * 256, O_PV + (g + 1) * 256), np.arange(O_PG + c * 128, O_PG + (c + 1) * 128),
                                   np.arange(O_X + 256 * c, O_X + 256 * (c + 1)), np.arange(O_B + 128 * g, O_B + 128 * (g + 1)),
                                   np.arange(O_C + 128 * g, O_C + 128 * (g + 1)), np.arange(O_Z + 256 * c, O_Z + 256 * (c + 1)),
                                   np.arange(O_DT + 4 * c, O_DT + 4 * (c + 1))])
            maps.append(dict(uT=uT, wB=np.ascontiguousarray(w_in[:, cols]), prew=prew, poolw=poolw[l, c], pvec=pvec[l, c],
                             convw=convw[l, c], rows=rows[l, c], consts=c2))
        r2 = run(progs['p2'], maps)
        mixB = np.stack([np.asarray(r2[c]['mixB']) for c in range(NCORES)])
        ssq = np.stack([np.asarray(r2[c]['ssq']) for c in range(NCORES)])
        wA = np.ascontiguousarray(w_in[:, colsA])
        wKV = np.ascontiguousarray(w_in[:, colsKV])
        r3 = run(progs['p3'], [dict(uT=uT, uTo=uTo[c], wA=wA, wKV=wKV, prew=prew, ropek=ropek,
                                    ropeq=np.ascontiguousarray(ropeq[own[c]]), qpos=qpos[c], kpos=kpos, consts=ident)
                               for c in range(NCORES)])
        attnT = [np.asarray(r3[c]['attnT']) for c in range(NCORES)]
        mixb5 = mixB.reshape(NCORES * 3, 128, NB, 128)
        wout = np.ascontiguousarray(inp['w_out'][l].reshape(32, 128, D))
        postw = np.ascontiguousarray(inp['post_norm_w'][l].reshape(1, D))
        maps = []
        for c in range(NCORES):
            mixo = np.ascontiguousarray(mixb5[:, :, own[c], :].transpose(2, 0, 1, 3))
            ssqo = np.ascontiguousarray(ssq[:, own[c], :].transpose(1, 0, 2))
            maps.append(dict(mixo=mixo, ssqo=ssqo, attnT=attnT[c], h=hown[c], wout=wout, postw=postw, rowvalid=rowvalid[c], consts=ident))
        r4 = run(progs['p4'], maps)
        hown = [np.asarray(r4[c]['hout']) for c in range(NCORES)]
    hfull = np.zeros((NB, 128, D), np.float32)
    for c in range(NCORES):
        hfull[own[c]] = hown[c]
    return hfull.reshape(PP, D)[128:][None]


def kernel(**inputs):
    inp = {k: np.asarray(v) for k, v in inputs.items()}
    S = inp['x'].shape[1]
    NB = (S + 128) // 128
    depth = inp['w_in'].shape[0]
    return forward_unfused(inp, NB, depth).astype(np.float32)
```
